# Optimizing a Trainium2 kernel written in Bass

```python
import jax
import jax.numpy as jnp
from jax import lax
import numpy as np

D_MODEL = 2048
BATCH = 16
SEQ = 2048
DEPTH = 4

CHUNK = 64
N_MEM = 256
RMS_EPS = 1e-6
MAX_STREAM_CHUNKS = 1024

GMLP_BLOCK = 128
A_WIDTH = D_MODEL // 2
A_GROUPS = 8
A_GROUP_DIM = A_WIDTH // A_GROUPS

B_WIDTH = D_MODEL - A_WIDTH
B_GROUPS = 8
CONV_WIDTH = 3
AB_IN = 2 * A_WIDTH + 3 * B_WIDTH

C_HEADS = D_MODEL // 128
C_NOPE = 128
C_ROPE = 64
C_V = 128
C_Q_RANK = 512
C_KV_RANK = 256
C_IN = C_Q_RANK + C_KV_RANK + C_ROPE
ROPE_THETA = 10000.0
Q_BLOCK = 128

MEM_HEADS = 4
MEM_HEAD_DIM = D_MODEL // MEM_HEADS

D_FF = 4 * D_MODEL

N_EVEN = (DEPTH + 1) // 2
N_ODD = DEPTH // 2

kernel_name = 'hybrid_gmlp_shortconv_mla_memxattn_trunk'


def rmsnorm(x, g):
    xf = x.astype(jnp.float32)
    y = xf * lax.rsqrt(jnp.mean(xf * xf, axis=-1, keepdims=True) + RMS_EPS)
    return (y * g.astype(jnp.float32)).astype(x.dtype)


def gmlp_spatial_gate(u, v, v_norm_g, w_s, b_s):
    bsz, seq, _ = u.shape
    nb = seq // GMLP_BLOCK
    vg = v.reshape(bsz, seq, A_GROUPS, A_GROUP_DIM)
    vg = rmsnorm(vg, v_norm_g.reshape(A_GROUPS, A_GROUP_DIM))
    vg = vg.reshape(bsz, nb, GMLP_BLOCK, A_GROUPS, A_GROUP_DIM)
    cid = jnp.arange(GMLP_BLOCK) // CHUNK
    mask = (cid[None, :] <= cid[:, None]).astype(w_s.dtype)
    w = w_s * mask[None]
    mixed = jnp.einsum('gij,bnjgc->bnigc', w, vg) + b_s.T[None, None, :, :, None]
    return u * mixed.reshape(bsz, seq, A_WIDTH)


def gated_short_conv(bg, cg, h, conv_w):
    z = cg * h
    seq = z.shape[1]
    zp = jnp.pad(z, ((0, 0), (CONV_WIDTH - 1, 0), (0, 0)))
    conv = sum(conv_w[k] * zp[:, k:k + seq] for k in range(CONV_WIDTH))
    return bg * conv


def mixer_gmlp_conv(xn, w_in, v_norm_g, w_s, b_s, conv_w, w_out):
    z = xn @ w_in
    u, v, bg, cg, h = jnp.split(
        z, [A_WIDTH, 2 * A_WIDTH, 2 * A_WIDTH + B_WIDTH, 2 * A_WIDTH + 2 * B_WIDTH], axis=-1)
    y_a = gmlp_spatial_gate(jax.nn.gelu(u), jax.nn.gelu(v), v_norm_g, w_s, b_s)
    y_b = gated_short_conv(bg, cg, h, conv_w)
    return jnp.concatenate([y_a, y_b], axis=-1) @ w_out


def rope_tables(positions):
    inv = ROPE_THETA ** (-jnp.arange(0, C_ROPE, 2, dtype=jnp.float32) / C_ROPE)
    ang = positions.astype(jnp.float32)[..., None] * inv
    return jnp.cos(ang), jnp.sin(ang)


def apply_rope(x, cos, sin):
    half = x.shape[-1] // 2
    x1 = x[..., :half].astype(jnp.float32)
    x2 = x[..., half:].astype(jnp.float32)
    return jnp.concatenate([x1 * cos - x2 * sin, x2 * cos + x1 * sin], axis=-1).astype(x.dtype)


def mixer_mla(xn, positions, w_in, q_norm_g, kv_norm_g, w_uq, w_ukv, w_out):
    bsz, seq, _ = xn.shape
    z = xn @ w_in
    c_q, c_kv, k_rope = jnp.split(z, [C_Q_RANK, C_Q_RANK + C_KV_RANK], axis=-1)
    q = (rmsnorm(c_q, q_norm_g) @ w_uq).reshape(bsz, seq, C_HEADS, C_NOPE + C_ROPE)
    kv = (rmsnorm(c_kv, kv_norm_g) @ w_ukv).reshape(bsz, seq, C_HEADS, C_NOPE + C_V)
    cos, sin = rope_tables(positions)
    q_nope = q[..., :C_NOPE]
    q_rope = apply_rope(q[..., C_NOPE:], cos[:, :, None], sin[:, :, None])
    k_nope = kv[..., :C_NOPE]
    v = kv[..., C_NOPE:]
    k_rope = apply_rope(k_rope, cos, sin)
    scale = (C_NOPE + C_ROPE) ** -0.5
    chunk_id = jnp.arange(seq) // CHUNK
    outs = []
    for qb in range(seq // Q_BLOCK):
        q0, q1 = qb * Q_BLOCK, (qb + 1) * Q_BLOCK
        s = (jnp.einsum('bqhd,bkhd->bhqk', q_nope[:, q0:q1], k_nope[:, :q1])
             + jnp.einsum('bqhr,bkr->bhqk', q_rope[:, q0:q1], k_rope[:, :q1]))
        s = s.astype(jnp.float32) * scale
        mask = chunk_id[None, :q1] <= chunk_id[q0:q1, None]
        s = jnp.where(mask[None, None], s, -jnp.inf)
        p = jax.nn.softmax(s, axis=-1).astype(v.dtype)
        outs.append(jnp.einsum('bhqk,bkhd->bqhd', p, v[:, :q1]))
    o = jnp.concatenate(outs, axis=1).reshape(bsz, seq, C_HEADS * C_V)
    return o @ w_out


def mem_cross_attention(xn, memn, wq, wk, wv, wo):
    bsz, seq, _ = xn.shape
    n_mem = memn.shape[1]
    q = (xn @ wq).reshape(bsz, seq, MEM_HEADS, MEM_HEAD_DIM)
    k = (memn @ wk).reshape(bsz, n_mem, MEM_HEADS, MEM_HEAD_DIM)
    v = (memn @ wv).reshape(bsz, n_mem, MEM_HEADS, MEM_HEAD_DIM)
    s = jnp.einsum('bqhd,bmhd->bhqm', q, k).astype(jnp.float32) * (MEM_HEAD_DIM ** -0.5)
    p = jax.nn.softmax(s, axis=-1).astype(v.dtype)
    o = jnp.einsum('bhqm,bmhd->bqhd', p, v).reshape(bsz, seq, D_MODEL)
    return o @ wo


def squared_relu_mlp(xn, w1, w2):
    h = jax.nn.relu(xn @ w1)
    return (h * h) @ w2


def _normal(key, shape, scale):
    return jax.random.normal(key, shape, jnp.float32) * scale


def _gain(key, shape):
    return 1.0 + 0.02 * jax.random.normal(key, shape, jnp.float32)


def setup_inputs(seed: int = 0) -> dict:
    key = jax.random.key(seed)
    ks = jax.random.split(key, 26)
    d = D_MODEL
    offset = jax.random.randint(ks[2], (BATCH, 1), 0, MAX_STREAM_CHUNKS, dtype=jnp.int32) * CHUNK
    positions = offset + jnp.arange(SEQ, dtype=jnp.int32)[None, :]
    return {
        'x': _normal(ks[0], (BATCH, SEQ, d), 1.0),
        'mem': _normal(ks[1], (BATCH, N_MEM, d), 1.0),
        'positions': positions,
        'norm_mix_g': _gain(ks[3], (DEPTH, d)),
        'norm_mem_q_g': _gain(ks[4], (DEPTH, d)),
        'norm_mem_kv_g': _gain(ks[5], (DEPTH, d)),
        'norm_ffn_g': _gain(ks[6], (DEPTH, d)),
        'final_norm_g': _gain(ks[7], (d,)),
        'ab_w_in': _normal(ks[8], (N_EVEN, d, AB_IN), d ** -0.5),
        'a_v_norm_g': _gain(ks[9], (N_EVEN, A_WIDTH)),
        'a_w_s': _normal(ks[10], (N_EVEN, A_GROUPS, GMLP_BLOCK, GMLP_BLOCK), GMLP_BLOCK ** -0.5),
        'a_b_s': _gain(ks[11], (N_EVEN, A_GROUPS, GMLP_BLOCK)),
        'b_conv_w': _normal(ks[12], (N_EVEN, CONV_WIDTH, B_WIDTH), CONV_WIDTH ** -0.5),
        'ab_w_out': _normal(ks[13], (N_EVEN, A_WIDTH + B_WIDTH, d), (A_WIDTH + B_WIDTH) ** -0.5),
        'c_w_in': _normal(ks[14], (N_ODD, d, C_IN), d ** -0.5),
        'c_q_norm_g': _gain(ks[15], (N_ODD, C_Q_RANK)),
        'c_kv_norm_g': _gain(ks[16], (N_ODD, C_KV_RANK)),
        'c_w_uq': _normal(ks[17], (N_ODD, C_Q_RANK, C_HEADS * (C_NOPE + C_ROPE)), C_Q_RANK ** -0.5),
        'c_w_ukv': _normal(ks[18], (N_ODD, C_KV_RANK, C_HEADS * (C_NOPE + C_V)), C_KV_RANK ** -0.5),
        'c_w_out': _normal(ks[19], (N_ODD, C_HEADS * C_V, d), (C_HEADS * C_V) ** -0.5),
        'm_wq': _normal(ks[20], (DEPTH, d, d), d ** -0.5),
        'm_wk': _normal(ks[21], (DEPTH, d, d), d ** -0.5),
        'm_wv': _normal(ks[22], (DEPTH, d, d), d ** -0.5),
        'm_wo': _normal(ks[23], (DEPTH, d, d), d ** -0.5),
        'f_w1': _normal(ks[24], (DEPTH, d, D_FF), d ** -0.5),
        'f_w2': _normal(ks[25], (DEPTH, D_FF, d), D_FF ** -0.5),
    }


def reference(x, mem, positions, norm_mix_g, norm_mem_q_g, norm_mem_kv_g, norm_ffn_g,
              final_norm_g, ab_w_in, a_v_norm_g, a_w_s, a_b_s, b_conv_w, ab_w_out,
              c_w_in, c_q_norm_g, c_kv_norm_g, c_w_uq, c_w_ukv, c_w_out,
              m_wq, m_wk, m_wv, m_wo, f_w1, f_w2):
    for layer in range(DEPTH):
        xn = rmsnorm(x, norm_mix_g[layer])
        if layer % 2 == 0:
            e = layer // 2
            x = x + mixer_gmlp_conv(xn, ab_w_in[e], a_v_norm_g[e], a_w_s[e], a_b_s[e],
                                    b_conv_w[e], ab_w_out[e])
        else:
            o = layer // 2
            x = x + mixer_mla(xn, positions, c_w_in[o], c_q_norm_g[o], c_kv_norm_g[o],
                              c_w_uq[o], c_w_ukv[o], c_w_out[o])
        x = x + mem_cross_attention(rmsnorm(x, norm_mem_q_g[layer]),
                                    rmsnorm(mem, norm_mem_kv_g[layer]),
                                    m_wq[layer], m_wk[layer], m_wv[layer], m_wo[layer])
        x = x + squared_relu_mlp(rmsnorm(x, norm_ffn_g[layer]), f_w1[layer], f_w2[layer])
    return rmsnorm(x, final_norm_g)
```

```python
from contextlib import ExitStack

import numpy as np
import concourse.bass as bass
import concourse.mybir as mybir
from concourse.bass_utils import run_bass_kernel_spmd

F32 = mybir.dt.float32
BF16 = mybir.dt.bfloat16
I32 = mybir.dt.int32
AF = mybir.ActivationFunctionType
ALU = mybir.AluOpType
AX = mybir.AxisListType

D = 2048
NCH = 16
TG = 512
SEQ = 2048
NMEM = 256
DFF = 8192
EPS = 1e-6
PI = float(np.pi)
NSLOT = 5

G_MIX, G_MEMQ, G_MEMKV, G_FFN, G_FINAL = 0, 64, 128, 192, 256
G_CQ, G_CKV, G_CONV = 272, 280, 284
G_ROWS = 332


class T:
    __slots__ = ("name", "w", "r")

    def __init__(self, name):
        self.name = name
        self.w = None
        self.r = {}


class Buf:
    def __init__(self, ap, tiles):
        self.ap = ap
        self.t = tiles


class Em:
    ENGS = ("pe", "act", "dve", "pool", "sp")

    def __init__(self, nc, es):
        self.nc = nc
        self.es = es
        self.streams = {e: [] for e in self.ENGS}
        self.sem = {}
        self.cnt = {}
        self.waited = {e: {} for e in self.ENGS}
        self.barrier_dma = set()
        for e in self.ENGS:
            self.newsem("eng_" + e)

    def newsem(self, key):
        self.sem[key] = self.es.enter_context(self.nc.semaphore(key))
        self.cnt[key] = 0
        return key

    def _deps(self, reads, writes):
        deps = []
        for t in reads:
            if t.w is not None:
                deps.append(t.w)
        for t in writes:
            if t.w is not None:
                deps.append(t.w)
            deps.extend(t.r.items())
        return deps

    def _wait(self, eng, deps):
        own = "eng_" + eng
        wd = self.waited[eng]
        need = {}
        for k, v in deps:
            if k == own and eng == "pe":
                continue
            if wd.get(k, 0) >= v:
                continue
            if need.get(k, 0) < v:
                need[k] = v
        for k, v in need.items():
            self.streams[eng].append(("wait", k, v))
            wd[k] = v

    def _mark(self, ev, reads, writes):
        for t in reads:
            if t.r.get(ev[0], 0) < ev[1]:
                t.r[ev[0]] = ev[1]
        for t in writes:
            t.w = ev
            t.r = {}

    def op(self, eng, fn, reads=(), writes=()):
        self._wait(eng, self._deps(reads, writes))
        k = "eng_" + eng
        self.cnt[k] += 1
        ev = (k, self.cnt[k])
        self.streams[eng].append(("op", fn, k, 1))
        self._mark(ev, reads, writes)
        return ev

    def mm(self, out_ap, pairs, reads, writes, pair_reads=None):
        self._wait("pe", self._deps(reads, writes))
        k = "eng_pe"
        n = len(pairs)
        allreads = list(reads)
        for i, (l, r) in enumerate(pairs):
            if pair_reads is not None:
                self._wait("pe", self._deps(pair_reads[i], ()))
                allreads.extend(pair_reads[i])
            last = i == n - 1
            fn = (lambda e, l=l, r=r, i=i, last=last: e.matmul(out_ap, lhsT=l, rhs=r, start=(i == 0), stop=last))
            if last:
                self.cnt[k] += 1
                self.streams["pe"].append(("op", fn, k, 1))
            else:
                self.streams["pe"].append(("op", fn, None, 0))
        ev = (k, self.cnt[k])
        self._mark(ev, allreads, writes)
        return ev

    def mm1(self, out_ap, l, r, start, stop, reads, writes):
        self._wait("pe", self._deps(reads, writes))
        k = "eng_pe"
        self.cnt[k] += 1
        self.streams["pe"].append(("op", lambda e: e.matmul(out_ap, lhsT=l, rhs=r, start=start, stop=stop), k, 1))
        ev = (k, self.cnt[k])
        self._mark(ev, reads, writes)
        return ev

    def tr(self, out_ap, in_ap, ident_ap, reads, writes):
        self._wait("pe", self._deps(reads, writes))
        k = "eng_pe"
        self.cnt[k] += 1
        self.streams["pe"].append(("op", lambda e: e.transpose(out_ap, in_ap, ident_ap), k, 1))
        ev = (k, self.cnt[k])
        self._mark(ev, reads, writes)
        return ev

    def dma(self, q, out_ap, in_ap, semkey, reads=(), writes=(), barrier=True, skip_own=False):
        deps = self._deps(reads, writes)
        if skip_own:
            deps = [d for d in deps if d[0] != semkey]
        self._wait(q, deps)
        self.cnt[semkey] += 16
        ev = (semkey, self.cnt[semkey])
        self.streams[q].append(("op", lambda e: e.dma_start(out=out_ap, in_=in_ap), semkey, 16))
        self._mark(ev, reads, writes)
        if barrier:
            self.barrier_dma.add(semkey)
        return ev

    def barrier(self, engs=("pe", "act", "dve", "sp")):
        evs = [("eng_" + e, self.cnt["eng_" + e]) for e in ("pe", "act", "dve")]
        evs += [(k, self.cnt[k]) for k in self.barrier_dma]
        for e in engs:
            self._wait(e, evs)

    def replay(self, block):
        em = self

        def run(handle, stream):
            for it in stream:
                if it[0] == "wait":
                    handle.wait_ge(em.sem[it[1]], it[2])
                else:
                    ins = it[1](handle)
                    if it[2] is not None:
                        ins.then_inc(em.sem[it[2]], it[3])

        @block.tensor
        def _(t):
            run(t, em.streams["pe"])

        @block.scalar
        def _(a):
            run(a, em.streams["act"])

        @block.vector
        def _(v):
            run(v, em.streams["dve"])

        @block.gpsimd
        def _(g):
            run(g, em.streams["pool"])

        @block.sync
        def _(s):
            run(s, em.streams["sp"])


def build_program(n_seq=2, n_tg=4, n_layers=4, phases="mcf", dbg=None):
    nc = bass.Bass("TRN2", target_bir_lowering=False)
    dt = lambda name, shape, d=F32, kind="ExternalInput": nc.dram_tensor(name, list(shape), d, kind=kind).ap()
    x_d = dt("x", [n_seq, SEQ, D])
    mem_d = dt("mem", [n_seq, NMEM, D])
    pos_d = dt("positions", [n_seq, SEQ], I32)
    gvec_d = dt("gvec", [G_ROWS, 128])
    ident_d = dt("ident", [128, 128])
    ropec_d = dt("ropec", [128, 2])
    ab_w_in = dt("ab_w_in", [2, D, 5120])
    a_v_norm_g = dt("a_v_norm_g", [2, 1024])
    a_w_s = dt("a_w_s", [2, 8, 128, 128])
    a_b_s = dt("a_b_s", [2, 1024])
    ab_w_out = dt("ab_w_out", [2, D, D])
    c_w_in = dt("c_w_in", [2, D, 832])
    c_w_uq = dt("c_w_uq", [2, 512, 3072])
    c_w_ukv = dt("c_w_ukv", [2, 256, 4096])
    c_w_out = dt("c_w_out", [2, D, D])
    m_wq = dt("m_wq", [4, D, D])
    m_wk = dt("m_wk", [4, D, D])
    m_wv = dt("m_wv", [4, D, D])
    m_wo = dt("m_wo", [4, D, D])
    f_w1 = dt("f_w1", [4, D, DFF])
    f_w2 = dt("f_w2", [4, DFF, D])
    out_d = dt("out", [n_seq, SEQ, D], F32, "ExternalOutput")
    dbg_d = dt("dbg", [16, 128, 512], F32, "ExternalOutput") if dbg else None
    ascr = [dt(f"ascr{l}", [4, 128, NCH * NMEM], BF16, "Internal") for l in range(4)]
    bscr = [dt(f"bscr{l}", [4, 128, 2 * D], BF16, "Internal") for l in range(4)]

    with ExitStack() as es:
        em = Em(nc, es)
        ctr = [0]

        def sb(es_, shape, d, name=None):
            ctr[0] += 1
            return es_.enter_context(nc.sbuf_tensor(f"{name or 't'}_{ctr[0]}", list(shape), d))

        def mkbuf(es_, shape, d, name, ntiles=None):
            t = sb(es_, shape, d, name)
            n = ntiles if ntiles is not None else 1
            return Buf(t, [T(f"{name}{i}") for i in range(n)])

        X = mkbuf(es, [128, NCH, TG], F32, "X", NCH)
        xn = mkbuf(es, [128, NCH, TG], BF16, "xn", NCH)
        Wr = [mkbuf(es, [128, 4096], BF16, f"W{i}") for i in range(NSLOT)]
        wsem = [em.newsem(f"wsem{i}") for i in range(NSLOT)]
        ckv = [mkbuf(es, [128, 2, SEQ], BF16, f"ckv{i}", 4) for i in range(2)]
        krc = [mkbuf(es, [128, SEQ], BF16, f"krc{i}", 4) for i in range(2)]
        memhT = mkbuf(es, [128, NCH, NMEM], BF16, "memhT")
        gT = mkbuf(es, [128, G_ROWS], F32, "gT")
        ident = mkbuf(es, [128, 128], F32, "ident")
        ones = mkbuf(es, [128, 128], BF16, "ones")
        identb = mkbuf(es, [128, 128], BF16, "identb")
        ropec = mkbuf(es, [128, 2], F32, "ropec")
        wsT = [mkbuf(es, [128, 8, 128], BF16, f"wsT{i}") for i in range(2)]
        Ct = mkbuf(es, [128, TG], F32, "Ct")
        St = mkbuf(es, [128, TG], F32, "St")
        rstd = mkbuf(es, [128, TG], F32, "rstd")
        r1 = mkbuf(es, [128, TG], F32, "r1")
        sq = [mkbuf(es, [128, TG], BF16, f"sq{i}") for i in range(4)]
        halo = [mkbuf(es, [128, 8, 2], F32, f"halo{i}") for i in range(2)]
        posi = mkbuf(es, [128, TG], I32, "posi")
        bsb = mkbuf(es, [128, 1024], F32, "bsb")
        vng = mkbuf(es, [128, 1024], F32, "vng")
        small = mkbuf(es, [128, 8], F32, "small")
        PS = [Buf(es.enter_context(nc.psum_tensor(f"ps{i}", [128, 512], F32)), [T(f"ps{i}")]) for i in range(8)]
        psi = [0]

        ps_pool = [list(range(7))]
        PSN = PS[7]

        def next_ps():
            pool_ = ps_pool[0]
            p = PS[pool_[psi[0] % len(pool_)]]
            psi[0] += 1
            return p

        s_in = [em.newsem(f"s_in{i}") for i in range(4)]
        s_out = [em.newsem(f"s_out{i}") for i in range(2)]
        s_misc = [em.newsem(f"s_misc{i}") for i in range(5)]
        s_pb = [em.newsem(f"s_pb{i}") for i in range(3)]
        wj = [0]
        evac_rr = [0]
        ascrT = [T(f"ascr{l}") for l in range(4)]
        bscrT = [T(f"bscr{l}") for l in range(4)]
        s_aw = [em.newsem(f"s_aw{l}") for l in range(4)]
        s_bw = [em.newsem(f"s_bw{l}") for l in range(4)]

        s_dbg = em.newsem("s_dbg")
        dbgt = mkbuf(es, [128, TG], F32, "dbgt") if dbg else None

        def dump(name, ap_of_chunk, tiles, n):
            if dbg != name:
                return
            for c in range(n):
                em.op("dve", lambda e, c=c: e.tensor_copy(out=dbgt.ap[:, :], in_=ap_of_chunk(c)), reads=list(tiles), writes=[dbgt.t[0]])
                em.dma("sp", dbg_d[c, :, :], dbgt.ap[:, :], s_dbg, reads=[dbgt.t[0]])

        def gcol(base, i=0):
            return gT.ap[:, base + i:base + i + 1]

        def evac_copy(out_ap, out_t, ps_ap, ps_t, eng=None):
            if eng is None:
                eng = "act" if evac_rr[0] % 2 == 0 else "dve"
                evac_rr[0] += 1
            if eng == "act":
                em.op("act", lambda e: e.copy(out=out_ap, in_=ps_ap), reads=[ps_t], writes=[out_t])
            else:
                em.op("dve", lambda e: e.tensor_copy(out=out_ap, in_=ps_ap), reads=[ps_t], writes=[out_t])

        def wslot():
            s = wj[0] % NSLOT
            wj[0] += 1
            return s

        def wblock_k2048(w2d, row0, col0, ncols=256):
            s = wslot()
            v = Wr[s].ap[:, :].rearrange("p (k f) -> p k f", f=256)
            for kq in range(4):
                src = w2d[row0 + kq * 512: row0 + (kq + 1) * 512, col0:col0 + ncols].rearrange("(k p) f -> p k f", p=128)
                em.dma("pool", v[:, kq * 4:(kq + 1) * 4, 0:ncols], src, wsem[s], writes=[Wr[s].t[0]], barrier=False, skip_own=True)
            return v, Wr[s].t[0]

        def wblock_generic(w2d, nk, col0, ncols):
            s = wslot()
            v = Wr[s].ap[:, 0:nk * ncols].rearrange("p (k f) -> p k f", f=ncols)
            src = w2d[:, col0:col0 + ncols].rearrange("(k p) f -> p k f", p=128)
            em.dma("pool", v, src, wsem[s], writes=[Wr[s].t[0]], barrier=False)
            return v, Wr[s].t[0]

        def norm_stats_chunk(src, c, nch, ps):
            s_ = sq[c % 4]
            st = src.t[c] if len(src.t) > 1 else src.t[0]
            em.op("act", lambda e, c=c, s_=s_: e.activation(out=s_.ap[:, :], in_=src.ap[:, c, :], func=AF.Square), reads=[st], writes=[s_.t[0]])
            em.mm1(ps.ap[:, :], ones.ap[:, :], s_.ap[:, :], c == 0, c == nch - 1, reads=[s_.t[0], ones.t[0]], writes=[ps.t[0]])

        def norm_finish(src, nch, dim, gbase, dst_tiles, dst_slicer, ps):
            em.op("act", lambda e: e.activation(out=r1.ap[:, :], in_=ps.ap[:, :], func=AF.Ln, scale=1.0 / dim, bias=EPS), reads=[ps.t[0]], writes=[r1.t[0]])
            em.op("act", lambda e: e.activation(out=rstd.ap[:, :], in_=r1.ap[:, :], func=AF.Exp, scale=-0.5), reads=[r1.t[0]], writes=[rstd.t[0]])
            for c in range(nch):
                em.op("dve", lambda e, c=c: e.scalar_tensor_tensor(out=dst_slicer(c), in0=src.ap[:, c, :], scalar=gcol(gbase, c), in1=rstd.ap[:, :],
                                                                   op0=ALU.mult, op1=ALU.mult),
                      reads=[src.t[c] if len(src.t) > 1 else src.t[0], rstd.t[0], gT.t[0]], writes=[dst_tiles[c] if len(dst_tiles) > 1 else dst_tiles[0]])

        def rmsnorm(src, nch, dim, gbase, dst, dst_tiles, dst_slicer):
            ps = next_ps()
            for c in range(nch):
                norm_stats_chunk(src, c, nch, ps)
            norm_finish(src, nch, dim, gbase, dst_tiles, dst_slicer, ps)

        nxt_norm = [None]
        prenormed = [False]

        pend_stats = []
        STATS_LAG = 3

        def x_chunk_final(dc):
            if nxt_norm[0] is not None:
                pend_stats.append(dc)
                while len(pend_stats) > STATS_LAG:
                    norm_stats_chunk(X, pend_stats.pop(0), NCH, PSN)

        def x_all_final():
            if nxt_norm[0] is None:
                return
            while pend_stats:
                norm_stats_chunk(X, pend_stats.pop(0), NCH, PSN)
            prenormed[0] = True

        def norm_x(gbase):
            if prenormed[0]:
                prenormed[0] = False
                norm_finish(X, NCH, D, gbase, xn.t, lambda c: xn.ap[:, c, :], PSN)
                return
            rmsnorm(X, NCH, D, gbase, xn, xn.t, lambda c: xn.ap[:, c, :])

        def proj_to_residual(w2d, src, nblk=8):
            for blk in range(nblk):
                wv, wt = wblock_k2048(w2d, 0, blk * 256)
                for j in range(2):
                    dc = blk * 2 + j
                    ps = next_ps()
                    em.mm(ps.ap[:, :], [(wv[:, kc, j * 128:(j + 1) * 128], src.ap[:, kc, :]) for kc in range(NCH)],
                          reads=[wt], pair_reads=[[src.t[kc]] for kc in range(NCH)], writes=[ps.t[0]])
                    em.op("dve", lambda e, dc=dc, ps=ps: e.tensor_tensor(out=X.ap[:, dc, :], in0=ps.ap[:, :], in1=X.ap[:, dc, :], op=ALU.add),
                          reads=[ps.t[0], X.t[dc]], writes=[X.t[dc]])
                    x_chunk_final(dc)
            x_all_final()

        em.dma("sp", ident.ap[:, :], ident_d[:, :], s_misc[3], writes=[ident.t[0]])
        em.dma("sp", ropec.ap[:, :], ropec_d[:, :], s_misc[4], writes=[ropec.t[0]])
        em.op("dve", lambda e: e.memset(ones.ap[:, :], 1.0), writes=[ones.t[0]])
        em.op("dve", lambda e: e.tensor_copy(out=identb.ap[:, :], in_=ident.ap[:, :]), reads=[ident.t[0]], writes=[identb.t[0]])
        with ExitStack() as ph:
            gst = [mkbuf(ph, [128, 128], F32, f"gst{i}") for i in range(3)]
            r0 = 0
            for i in range(3):
                nr = min(128, G_ROWS - r0)
                if nr < 128:
                    em.op("dve", lambda e, i=i: e.memset(gst[i].ap[:, :], 0.0), writes=[gst[i].t[0]])
                em.dma("sp", gst[i].ap[0:nr, :], gvec_d[r0:r0 + nr, :], s_misc[i], writes=[gst[i].t[0]])
                ps = next_ps()
                em.tr(ps.ap[:, 0:128], gst[i].ap[:, :], ident.ap[:, :], reads=[gst[i].t[0], ident.t[0]], writes=[ps.t[0]])
                em.op("dve", lambda e, ps=ps, r0=r0, nr=nr: e.tensor_copy(out=gT.ap[:, r0:r0 + nr], in_=ps.ap[:, 0:nr]), reads=[ps.t[0]], writes=[gT.t[0]])
                r0 += nr
            for e_ in range(2):
                for g in range(8):
                    st_ = gst[g % 3]
                    em.dma("sp", st_.ap[:, :], a_w_s[e_, g, :, :], s_misc[g % 3], writes=[st_.t[0]])
                    ps = next_ps()
                    em.tr(ps.ap[:, 0:128], st_.ap[:, :], ident.ap[:, :], reads=[st_.t[0], ident.t[0]], writes=[ps.t[0]])
                    em.op("act", lambda e, ps=ps, e_=e_, g=g: e.copy(out=wsT[e_].ap[:, g, :], in_=ps.ap[:, 0:128]), reads=[ps.t[0]], writes=[wsT[e_].t[0]])
                em.op("dve", lambda e, e_=e_: e.memset(wsT[e_].ap[64:128, :, 0:64], 0.0), writes=[wsT[e_].t[0]])
            em.barrier()

        def ffn(l):
            with ExitStack() as ph:
                hT = [mkbuf(ph, [128, NCH, TG], BF16, f"hT{i}", NCH) for i in range(2)]
                rs = [mkbuf(ph, [128, TG], F32, f"rs{i}") for i in range(2)]
                norm_x(G_FFN + l * 16)
                k_ = [0]

                def w1(q):
                    h = hT[q % 2]
                    for fb in range(8):
                        wv, wt = wblock_k2048(f_w1[l], 0, q * 2048 + fb * 256)
                        for j in range(2):
                            fc = fb * 2 + j
                            ps = next_ps()
                            em.mm(ps.ap[:, :], [(wv[:, kc, j * 128:(j + 1) * 128], xn.ap[:, kc, :]) for kc in range(NCH)],
                                  reads=[wt], pair_reads=[[xn.t[kc]] for kc in range(NCH)], writes=[ps.t[0]])
                            r = rs[k_[0] % 2]
                            k_[0] += 1
                            em.op("act", lambda e, ps=ps, r=r: e.activation(out=r.ap[:, :], in_=ps.ap[:, :], func=AF.Relu), reads=[ps.t[0]], writes=[r.t[0]])
                            em.op("dve", lambda e, r=r, h=h, fc=fc: e.tensor_tensor(out=h.ap[:, fc, :], in0=r.ap[:, :], in1=r.ap[:, :], op=ALU.mult),
                                  reads=[r.t[0]], writes=[h.t[fc]])

                def w2(q):
                    h = hT[q % 2]
                    for db in range(8):
                        wv, wt = wblock_k2048(f_w2[l], q * 2048, db * 256)
                        for j in range(2):
                            dc = db * 2 + j
                            ps = next_ps()
                            em.mm(ps.ap[:, :], [(wv[:, kc, j * 128:(j + 1) * 128], h.ap[:, kc, :]) for kc in range(NCH)],
                                  reads=[wt], pair_reads=[[h.t[kc]] for kc in range(NCH)], writes=[ps.t[0]])
                            em.op("dve", lambda e, dc=dc, ps=ps: e.tensor_tensor(out=X.ap[:, dc, :], in0=ps.ap[:, :], in1=X.ap[:, dc, :], op=ALU.add),
                                  reads=[ps.t[0], X.t[dc]], writes=[X.t[dc]])
                            if q == 3:
                                x_chunk_final(dc)

                w1(0)
                for q in range(4):
                    if q + 1 < 4:
                        w1(q + 1)
                    w2(q)
                x_all_final()
                em.barrier()

        def cross(l, tg):
            if tg == 0:
                with ExitStack() as ph:
                    memn = mkbuf(ph, [128, NCH, NMEM], BF16, "memn")
                    kT = mkbuf(ph, [128, NCH, NMEM], BF16, "kT")
                    vT = mkbuf(ph, [128, NCH, NMEM], BF16, "vT")
                    wqT = mkbuf(ph, [128, 4, D], BF16, "wqT")
                    Asb = mkbuf(ph, [128, NCH, NMEM], BF16, "Asb")
                    Bsb = mkbuf(ph, [128, 2, D], BF16, "Bsb")
                    for c in range(NCH):
                        em.op("dve", lambda e, c=c: e.tensor_scalar(out=memn.ap[:, c, :], in0=memhT.ap[:, c, :], scalar1=gcol(G_MEMKV + l * 16, c), scalar2=None, op0=ALU.mult),
                              reads=[memhT.t[0], gT.t[0]], writes=[memn.t[0]])
                    for (wmat, dstb) in ((m_wk[l], kT), (m_wv[l], vT)):
                        for blk in range(8):
                            wv, wt = wblock_k2048(wmat, 0, blk * 256)
                            for j in range(2):
                                fc = blk * 2 + j
                                ps = next_ps()
                                em.mm(ps.ap[:, 0:NMEM], [(wv[:, kc, j * 128:(j + 1) * 128], memn.ap[:, kc, :]) for kc in range(NCH)],
                                      reads=[wt, memn.t[0]], writes=[ps.t[0]])
                                evac_copy(dstb.ap[:, fc, :], dstb.t[0], ps.ap[:, 0:NMEM], ps.t[0])
                    for h in range(4):
                        for b2 in range(2):
                            wv, wt = wblock_k2048(m_wq[l], 0, (h * 2 + b2) * 256)
                            for j in range(2):
                                jc = b2 * 2 + j
                                for d4 in range(4):
                                    ps = next_ps()
                                    for di in range(4):
                                        dc = d4 * 4 + di
                                        em.mm1(ps.ap[:, di * 128:(di + 1) * 128], wv[:, dc, j * 128:(j + 1) * 128], identb.ap[:, :], True, True,
                                               reads=[wt, identb.t[0]], writes=[ps.t[0]])
                                    evac_copy(wqT.ap[:, jc, d4 * 512:(d4 + 1) * 512], wqT.t[0], ps.ap[:, :], ps.t[0])
                        for d2 in range(8):
                            ps = next_ps()
                            for di in range(2):
                                dc = d2 * 2 + di
                                em.mm(ps.ap[:, di * 256:(di + 1) * 256], [(wqT.ap[:, jc, dc * 128:(dc + 1) * 128], kT.ap[:, 4 * h + jc, :]) for jc in range(4)],
                                      reads=[wqT.t[0], kT.t[0]], writes=[ps.t[0]])
                            evac_copy(Asb.ap[:, d2 * 2:d2 * 2 + 2, :], Asb.t[0], ps.ap[:, :].rearrange("p (c m) -> p c m", m=256), ps.t[0])
                        em.dma("sp", ascr[l][h, :, :], Asb.ap[:, :, :].rearrange("p c m -> p (c m)"), s_aw[l], reads=[Asb.t[0]], writes=[ascrT[l]])
                        for half in range(2):
                            s_ = wslot()
                            wv = Wr[s_].ap[:, :].rearrange("p (k f) -> p k f", f=1024)
                            src = m_wo[l][h * 512:(h + 1) * 512, half * 1024:(half + 1) * 1024].rearrange("(k p) f -> p k f", p=128)
                            em.dma("pool", wv, src, wsem[s_], writes=[Wr[s_].t[0]], barrier=False, skip_own=True)
                            wt = Wr[s_].t[0]
                            for mt in range(2):
                                for dg in range(2):
                                    ps = next_ps()
                                    em.mm(ps.ap[:, :], [(vT.ap[:, 4 * h + jc, mt * 128:(mt + 1) * 128], wv[:, jc, dg * 512:(dg + 1) * 512]) for jc in range(4)],
                                          reads=[wt, vT.t[0]], writes=[ps.t[0]])
                                    d0 = half * 1024 + dg * 512
                                    evac_copy(Bsb.ap[:, mt, d0:d0 + 512], Bsb.t[0], ps.ap[:, :], ps.t[0])
                        em.dma("sp", bscr[l][h, :, :], Bsb.ap[:, :, :].rearrange("p t f -> p (t f)"), s_bw[l], reads=[Bsb.t[0]], writes=[bscrT[l]])
                    em.barrier()
            with ExitStack() as ph:
                pT = [mkbuf(ph, [128, 2, TG], BF16, f"pT{i}") for i in range(2)]
                pn = mkbuf(ph, [128, 8, TG], BF16, "pn", 8)
                rden = mkbuf(ph, [128, TG], F32, "rden")
                norm_x(G_MEMQ + l * 16)
                scale = float(512 ** -0.5)
                for h in range(4):
                    s_ = wslot()
                    av = Wr[s_].ap[:, :].rearrange("p (c m) -> p c m", m=256)
                    em.dma("pool", Wr[s_].ap[:, :], ascr[l][h, :, :], wsem[s_], reads=[ascrT[l]], writes=[Wr[s_].t[0]], barrier=False, skip_own=True)
                    wt = Wr[s_].t[0]
                    p_ = pT[h % 2]
                    for mt in range(2):
                        ps = next_ps()
                        em.mm(ps.ap[:, :], [(av[:, dc, mt * 128:(mt + 1) * 128], xn.ap[:, dc, :]) for dc in range(NCH)],
                              reads=[wt], pair_reads=[[xn.t[dc]] for dc in range(NCH)], writes=[ps.t[0]])
                        em.op("act", lambda e, ps=ps, p_=p_, mt=mt: e.activation(out=p_.ap[:, mt, :], in_=ps.ap[:, :], func=AF.Exp, scale=scale),
                              reads=[ps.t[0]], writes=[p_.t[0]])
                    psd = next_ps()
                    em.mm(psd.ap[:, :], [(ones.ap[:, :], p_.ap[:, mt, :]) for mt in range(2)], reads=[ones.t[0], p_.t[0]], writes=[psd.t[0]])
                    em.op("dve", lambda e, psd=psd: e.reciprocal(out=rden.ap[:, :], in_=psd.ap[:, :]), reads=[psd.t[0]], writes=[rden.t[0]])
                    for mt in range(2):
                        em.op("dve", lambda e, p_=p_, mt=mt, h=h: e.tensor_tensor(out=pn.ap[:, 2 * h + mt, :], in0=p_.ap[:, mt, :], in1=rden.ap[:, :], op=ALU.mult),
                              reads=[p_.t[0], rden.t[0]], writes=[pn.t[2 * h + mt]])
                bview = bscr[l][:, :, :].rearrange("h p (t f) -> p h t f", t=2)
                for dblk in range(8):
                    s_ = wslot()
                    wv = Wr[s_].ap[:, 0:8 * 256].rearrange("p (k f) -> p k f", f=256)
                    wv4 = Wr[s_].ap[:, 0:8 * 256].rearrange("p (h t f) -> p h t f", t=2, f=256)
                    for h4 in range(4):
                        em.dma("pool", wv4[:, h4, :, :], bview[:, h4, :, dblk * 256:(dblk + 1) * 256], wsem[s_],
                               reads=[bscrT[l]], writes=[Wr[s_].t[0]], barrier=False, skip_own=True)
                    wt = Wr[s_].t[0]
                    for j in range(2):
                        dc = dblk * 2 + j
                        ps = next_ps()
                        em.mm(ps.ap[:, :], [(wv[:, k8, j * 128:(j + 1) * 128], pn.ap[:, k8, :]) for k8 in range(8)],
                              reads=[wt], pair_reads=[[pn.t[k8]] for k8 in range(8)], writes=[ps.t[0]])
                        em.op("dve", lambda e, dc=dc, ps=ps: e.tensor_tensor(out=X.ap[:, dc, :], in0=ps.ap[:, :], in1=X.ap[:, dc, :], op=ALU.add),
                              reads=[ps.t[0], X.t[dc]], writes=[X.t[dc]])
                        x_chunk_final(dc)
                x_all_final()
                em.barrier()

        def mixer_even(l, tg):
            e_ = l // 2
            W = ab_w_in[e_]
            with ExitStack() as ph:
                Bz = mkbuf(ph, [128, 8, TG + 2], F32, "Bz", 8)
                vn = mkbuf(ph, [128, 4, 1024], BF16, "vn", 4)
                y = mkbuf(ph, [128, NCH, TG], BF16, "y", NCH)
                vs = [mkbuf(ph, [128, 256], F32, f"vs{i}") for i in range(2)]
                vq = [mkbuf(ph, [128, 256], F32, f"vq{i}") for i in range(2)]
                gu = [mkbuf(ph, [128, TG], F32, f"gu{i}") for i in range(2)]
                tm = [mkbuf(ph, [128, TG], F32, f"tm{i}") for i in range(2)]
                ss = [mkbuf(ph, [128, 2], F32, f"ss{i}") for i in range(2)]
                em.dma("pool", bsb.ap[:, :], a_b_s[e_:e_ + 1, :].partition_broadcast(128), s_pb[0], writes=[bsb.t[0]])
                em.dma("pool", vng.ap[:, :], a_v_norm_g[e_:e_ + 1, :].partition_broadcast(128), s_pb[1], writes=[vng.t[0]])
                norm_x(G_MIX + l * 16)
                if tg == 0:
                    em.op("dve", lambda e: e.memset(Bz.ap[:, :, 0:2], 0.0), writes=Bz.t)
                else:
                    em.op("dve", lambda e: e.tensor_copy(out=Bz.ap[:, :, 0:2], in_=halo[e_].ap[:, :, :]), reads=[halo[e_].t[0]], writes=Bz.t)
                k_ = 0
                for vb in range(4):
                    wv, wt = wblock_k2048(W, 0, 1024 + vb * 256)
                    for tt in range(4):
                        ps = next_ps()
                        em.mm(ps.ap[:, 0:256], [(xn.ap[:, kc, tt * 128:(tt + 1) * 128], wv[:, kc, :]) for kc in range(NCH)],
                              reads=[wt], pair_reads=[[xn.t[kc]] for kc in range(NCH)], writes=[ps.t[0]])
                        v_ = vs[k_ % 2]
                        q_ = vq[k_ % 2]
                        s_ = ss[k_ % 2]
                        k_ += 1
                        em.op("act", lambda e, ps=ps, v_=v_: e.activation(out=v_.ap[:, :], in_=ps.ap[:, 0:256], func=AF.Gelu_apprx_tanh), reads=[ps.t[0]], writes=[v_.t[0]])
                        em.op("dve", lambda e, v_=v_, q_=q_: e.tensor_tensor(out=q_.ap[:, :], in0=v_.ap[:, :], in1=v_.ap[:, :], op=ALU.mult), reads=[v_.t[0]], writes=[q_.t[0]])
                        em.op("dve", lambda e, q_=q_, s_=s_: e.tensor_reduce(out=s_.ap[:, :], in_=q_.ap[:, :].rearrange("p (g c) -> p g c", c=128), axis=AX.X, op=ALU.add),
                              reads=[q_.t[0]], writes=[s_.t[0]])
                        em.op("dve", lambda e, s_=s_: e.tensor_scalar(out=s_.ap[:, :], in0=s_.ap[:, :], scalar1=1.0 / 128, scalar2=EPS, op0=ALU.mult, op1=ALU.add),
                              reads=[s_.t[0]], writes=[s_.t[0]])
                        em.op("act", lambda e, s_=s_: e.activation(out=s_.ap[:, :], in_=s_.ap[:, :], func=AF.Sqrt), reads=[s_.t[0]], writes=[s_.t[0]])
                        em.op("dve", lambda e, s_=s_: e.reciprocal(out=s_.ap[:, :], in_=s_.ap[:, :]), reads=[s_.t[0]], writes=[s_.t[0]])
                        for g2 in range(2):
                            c0 = vb * 256 + g2 * 128
                            em.op("dve", lambda e, v_=v_, s_=s_, g2=g2, c0=c0, tt=tt: e.scalar_tensor_tensor(
                                out=vn.ap[:, tt, c0:c0 + 128], in0=v_.ap[:, g2 * 128:(g2 + 1) * 128], scalar=s_.ap[:, g2:g2 + 1], in1=vng.ap[:, c0:c0 + 128],
                                op0=ALU.mult, op1=ALU.mult), reads=[v_.t[0], s_.t[0], vng.t[0]], writes=[vn.t[tt]])
                k_ = 0
                for ub in range(4):
                    wv, wt = wblock_k2048(W, 0, ub * 256)
                    for j in range(2):
                        g = ub * 2 + j
                        psu = next_ps()
                        em.mm(psu.ap[:, :], [(wv[:, kc, j * 128:(j + 1) * 128], xn.ap[:, kc, :]) for kc in range(NCH)],
                              reads=[wt], pair_reads=[[xn.t[kc]] for kc in range(NCH)], writes=[psu.t[0]])
                        psm = next_ps()
                        for tt in range(4):
                            em.mm1(psm.ap[:, tt * 128:(tt + 1) * 128], vn.ap[:, tt, g * 128:(g + 1) * 128], wsT[e_].ap[:, g, :], True, True,
                                   reads=[vn.t[tt], wsT[e_].t[0]], writes=[psm.t[0]])
                        gu_ = gu[k_ % 2]
                        tm_ = tm[k_ % 2]
                        k_ += 1
                        em.op("act", lambda e, psu=psu, gu_=gu_: e.activation(out=gu_.ap[:, :], in_=psu.ap[:, :], func=AF.Gelu_apprx_tanh), reads=[psu.t[0]], writes=[gu_.t[0]])
                        em.op("dve", lambda e, psm=psm, tm_=tm_, g=g: e.tensor_tensor(
                            out=tm_.ap[:, :].rearrange("p (t i) -> p t i", i=128), in0=psm.ap[:, :].rearrange("p (t i) -> p t i", i=128),
                            in1=bsb.ap[:, g * 128:(g + 1) * 128].unsqueeze(1).to_broadcast([128, 4, 128]), op=ALU.add),
                            reads=[psm.t[0], bsb.t[0]], writes=[tm_.t[0]])
                        em.op("dve", lambda e, tm_=tm_, gu_=gu_, g=g: e.tensor_tensor(out=y.ap[:, g, :], in0=tm_.ap[:, :], in1=gu_.ap[:, :], op=ALU.mult),
                              reads=[tm_.t[0], gu_.t[0]], writes=[y.t[g]])
                for cb in range(4):
                    wv, wt = wblock_k2048(W, 0, 3072 + cb * 256)
                    for j in range(2):
                        c = cb * 2 + j
                        ps = next_ps()
                        em.mm(ps.ap[:, :], [(wv[:, kc, j * 128:(j + 1) * 128], xn.ap[:, kc, :]) for kc in range(NCH)],
                              reads=[wt], pair_reads=[[xn.t[kc]] for kc in range(NCH)], writes=[ps.t[0]])
                        em.op("act", lambda e, ps=ps, c=c: e.copy(out=Bz.ap[:, c, 2:TG + 2], in_=ps.ap[:, :]), reads=[ps.t[0]], writes=[Bz.t[c]])
                for hb in range(4):
                    wv, wt = wblock_k2048(W, 0, 4096 + hb * 256)
                    for j in range(2):
                        c = hb * 2 + j
                        ps = next_ps()
                        em.mm(ps.ap[:, :], [(wv[:, kc, j * 128:(j + 1) * 128], xn.ap[:, kc, :]) for kc in range(NCH)],
                              reads=[wt], pair_reads=[[xn.t[kc]] for kc in range(NCH)], writes=[ps.t[0]])
                        em.op("dve", lambda e, ps=ps, c=c: e.tensor_tensor(out=Bz.ap[:, c, 2:TG + 2], in0=ps.ap[:, :], in1=Bz.ap[:, c, 2:TG + 2], op=ALU.mult),
                              reads=[ps.t[0], Bz.t[c]], writes=[Bz.t[c]])
                em.op("dve", lambda e: e.tensor_copy(out=halo[e_].ap[:, :, :], in_=Bz.ap[:, :, TG:TG + 2]), reads=Bz.t, writes=[halo[e_].t[0]])
                k_ = 0
                for bb in range(4):
                    wv, wt = wblock_k2048(W, 0, 2048 + bb * 256)
                    for j in range(2):
                        c = bb * 2 + j
                        ps = next_ps()
                        em.mm(ps.ap[:, :], [(wv[:, kc, j * 128:(j + 1) * 128], xn.ap[:, kc, :]) for kc in range(NCH)],
                              reads=[wt], pair_reads=[[xn.t[kc]] for kc in range(NCH)], writes=[ps.t[0]])
                        t_ = tm[k_ % 2]
                        k_ += 1
                        cw = lambda k, c=c: gcol(G_CONV + (e_ * 3 + k) * 8, c)
                        em.op("dve", lambda e, t_=t_, c=c, cw=cw: e.tensor_scalar(out=t_.ap[:, :], in0=Bz.ap[:, c, 0:TG], scalar1=cw(0), scalar2=None, op0=ALU.mult),
                              reads=[Bz.t[c], gT.t[0]], writes=[t_.t[0]])
                        em.op("dve", lambda e, t_=t_, c=c, cw=cw: e.scalar_tensor_tensor(out=t_.ap[:, :], in0=Bz.ap[:, c, 1:TG + 1], scalar=cw(1), in1=t_.ap[:, :], op0=ALU.mult, op1=ALU.add),
                              reads=[Bz.t[c], gT.t[0], t_.t[0]], writes=[t_.t[0]])
                        em.op("dve", lambda e, t_=t_, c=c, cw=cw: e.scalar_tensor_tensor(out=t_.ap[:, :], in0=Bz.ap[:, c, 2:TG + 2], scalar=cw(2), in1=t_.ap[:, :], op0=ALU.mult, op1=ALU.add),
                              reads=[Bz.t[c], gT.t[0], t_.t[0]], writes=[t_.t[0]])
                        em.op("dve", lambda e, t_=t_, c=c, ps=ps: e.tensor_tensor(out=y.ap[:, 8 + c, :], in0=ps.ap[:, :], in1=t_.ap[:, :], op=ALU.mult),
                              reads=[ps.t[0], t_.t[0]], writes=[y.t[8 + c]])
                dump("y", lambda c: y.ap[:, c, :], y.t, 16)
                dump("vn", lambda c: vn.ap[:, c // 2, (c % 2) * 512:(c % 2) * 512 + 512], vn.t, 8)
                proj_to_residual(ab_w_out[e_], y)
                em.barrier()

        def mixer_odd(l, tg):
            o_ = l // 2
            W = c_w_in[o_]
            tok0 = tg * TG
            scale = float(192 ** -0.5)
            with ExitStack() as ph:
                cqn = mkbuf(ph, [128, 4, TG], BF16, "cqn", 4)
                oT = mkbuf(ph, [128, NCH, TG], BF16, "oT", NCH)
                with ExitStack() as ph2:
                    cq = mkbuf(ph2, [128, 4, TG], F32, "cq", 4)
                    ckvf = mkbuf(ph2, [128, 2, TG], F32, "ckvf", 2)
                    tk = [mkbuf(ph2, [128, TG], F32, f"tk{i}") for i in range(2)]
                    norm_x(G_MIX + l * 16)
                    for blk in range(3):
                        wv, wt = wblock_k2048(W, 0, blk * 256)
                        for j in range(2):
                            ps = next_ps()
                            em.mm(ps.ap[:, :], [(wv[:, kc, j * 128:(j + 1) * 128], xn.ap[:, kc, :]) for kc in range(NCH)],
                                  reads=[wt], pair_reads=[[xn.t[kc]] for kc in range(NCH)], writes=[ps.t[0]])
                            if blk < 2:
                                c = blk * 2 + j
                                evac_copy(cq.ap[:, c, :], cq.t[c], ps.ap[:, :], ps.t[0])
                            else:
                                evac_copy(ckvf.ap[:, j, :], ckvf.t[j], ps.ap[:, :], ps.t[0])
                    wv, wt = wblock_k2048(W, 0, 768, ncols=64)
                    em.op("act", lambda e, wv=wv: e.copy(out=wv[:, :, 64:128], in_=wv[:, :, 0:64]), reads=[wt], writes=[wt])
                    em.op("dve", lambda e, wv=wv: e.tensor_copy(out=wv[:, :, 128:160], in_=wv[:, :, 32:64]), reads=[wt], writes=[wt])
                    em.op("dve", lambda e, wv=wv: e.tensor_copy(out=wv[:, :, 160:192], in_=wv[:, :, 0:32]), reads=[wt], writes=[wt])
                    em.op("act", lambda e, wv=wv: e.copy(out=wv[:, :, 192:256], in_=wv[:, :, 128:192]), reads=[wt], writes=[wt])
                    psk = next_ps()
                    em.mm(psk.ap[:, :], [(wv[:, kc, 0:128], xn.ap[:, kc, :]) for kc in range(NCH)], reads=[wt], pair_reads=[[xn.t[kc]] for kc in range(NCH)], writes=[psk.t[0]])
                    pskp = next_ps()
                    em.mm(pskp.ap[:, :], [(wv[:, kc, 128:256], xn.ap[:, kc, :]) for kc in range(NCH)], reads=[wt], pair_reads=[[xn.t[kc]] for kc in range(NCH)], writes=[pskp.t[0]])
                    em.op("dve", lambda e: e.tensor_tensor(out=tk[0].ap[:, :], in0=pskp.ap[:, :], in1=St.ap[:, :], op=ALU.mult), reads=[pskp.t[0], St.t[0]], writes=[tk[0].t[0]])
                    em.op("dve", lambda e: e.tensor_tensor(out=tk[1].ap[:, :], in0=psk.ap[:, :], in1=Ct.ap[:, :], op=ALU.mult), reads=[psk.t[0], Ct.t[0]], writes=[tk[1].t[0]])
                    em.op("dve", lambda e: e.tensor_tensor(out=krc[o_].ap[:, tok0:tok0 + TG], in0=tk[0].ap[:, :], in1=tk[1].ap[:, :], op=ALU.add),
                          reads=[tk[0].t[0], tk[1].t[0]], writes=[krc[o_].t[tg]])
                    rmsnorm(cq, 4, 512, G_CQ + o_ * 4, cqn, cqn.t, lambda c: cqn.ap[:, c, :])
                    rmsnorm(ckvf, 2, 256, G_CKV + o_ * 2, ckv[o_], [ckv[o_].t[tg]], lambda c: ckv[o_].ap[:, c, tok0:tok0 + TG])
                    em.barrier()
                with ExitStack() as ph2:
                    Kx = [mkbuf(ph2, [128, SEQ], BF16, f"Kx{i}") for i in range(2)]
                    Vx = mkbuf(ph2, [128, 16, 256], BF16, "Vx")
                    qn = [mkbuf(ph2, [128, TG], BF16, f"qn{i}") for i in range(2)]
                    qr = mkbuf(ph2, [128, TG], BF16, "qr")
                    wr = mkbuf(ph2, [128, 4, 128], BF16, "wr")
                    wp = mkbuf(ph2, [128, 4, 128], BF16, "wp")
                    pTs = [mkbuf(ph2, [128, TG], BF16, f"pTs{i}") for i in range(4)]
                    rden = mkbuf(ph2, [128, TG], F32, "rden")
                    t2 = [mkbuf(ph2, [128, TG], F32, f"t2{i}") for i in range(2)]
                    nkt = 4 * (tg + 1)
                    pi_ = [0]
                    ps_pool[0] = [0, 1, 2, 7]
                    for hp in range(8):
                        wv, wt = wblock_generic(c_w_uq[o_], 4, hp * 384, 384)
                        wv4 = wv.rearrange("p k (h c) -> p k h c", c=192)
                        em.op("act", lambda e, wv4=wv4: e.copy(out=wr.ap[:, :, :].rearrange("p k (h c) -> p k h c", c=64), in_=wv4[:, :, :, 128:192]),
                              reads=[wt], writes=[wr.t[0]])
                        wp5 = wp.ap[:, :, :].rearrange("p k (h s c) -> p k h s c", s=2, c=32)
                        em.op("dve", lambda e, wv4=wv4, wp5=wp5: e.tensor_copy(out=wp5[:, :, :, 0, :], in_=wv4[:, :, :, 160:192]), reads=[wt], writes=[wp.t[0]])
                        em.op("dve", lambda e, wv4=wv4, wp5=wp5: e.tensor_copy(out=wp5[:, :, :, 1, :], in_=wv4[:, :, :, 128:160]), reads=[wt], writes=[wp.t[0]])
                        for hh in range(2):
                            ps = next_ps()
                            em.mm(ps.ap[:, :], [(wv[:, kc, hh * 192:hh * 192 + 128], cqn.ap[:, kc, :]) for kc in range(4)], reads=[wt], pair_reads=[[cqn.t[kc]] for kc in range(4)], writes=[ps.t[0]])
                            evac_copy(qn[hh].ap[:, :], qn[hh].t[0], ps.ap[:, :], ps.t[0])
                        psr = next_ps()
                        em.mm(psr.ap[:, :], [(wr.ap[:, kc, :], cqn.ap[:, kc, :]) for kc in range(4)], reads=[wr.t[0]], pair_reads=[[cqn.t[kc]] for kc in range(4)], writes=[psr.t[0]])
                        psp = next_ps()
                        em.mm(psp.ap[:, :], [(wp.ap[:, kc, :], cqn.ap[:, kc, :]) for kc in range(4)], reads=[wp.t[0]], pair_reads=[[cqn.t[kc]] for kc in range(4)], writes=[psp.t[0]])
                        em.op("dve", lambda e, psp=psp: e.tensor_tensor(out=t2[0].ap[:, :], in0=psp.ap[:, :], in1=St.ap[:, :], op=ALU.mult), reads=[psp.t[0], St.t[0]], writes=[t2[0].t[0]])
                        em.op("dve", lambda e, psr=psr: e.tensor_tensor(out=t2[1].ap[:, :], in0=psr.ap[:, :], in1=Ct.ap[:, :], op=ALU.mult), reads=[psr.t[0], Ct.t[0]], writes=[t2[1].t[0]])
                        em.op("dve", lambda e: e.tensor_tensor(out=qr.ap[:, :], in0=t2[0].ap[:, :], in1=t2[1].ap[:, :], op=ALU.add), reads=[t2[0].t[0], t2[1].t[0]], writes=[qr.t[0]])
                        wk, wkt = wblock_generic(c_w_ukv[o_], 2, hp * 512, 512)
                        for hh in range(2):
                            for tc in range(tg + 1):
                                ps = next_ps()
                                em.mm(ps.ap[:, :], [(wk[:, kc, hh * 256:hh * 256 + 128], ckv[o_].ap[:, kc, tc * TG:(tc + 1) * TG]) for kc in range(2)],
                                      reads=[wkt, ckv[o_].t[tc]], writes=[ps.t[0]])
                                evac_copy(Kx[hh].ap[:, tc * TG:(tc + 1) * TG], Kx[hh].t[0], ps.ap[:, :], ps.t[0])
                        for t4 in range(nkt // 2):
                            ps = next_ps()
                            for sub in range(2):
                                tkk = t4 * 2 + sub
                                for hh in range(2):
                                    em.mm(ps.ap[:, sub * 256 + hh * 128: sub * 256 + (hh + 1) * 128],
                                          [(ckv[o_].ap[:, kc, tkk * 128:(tkk + 1) * 128], wk[:, kc, hh * 256 + 128:hh * 256 + 256]) for kc in range(2)],
                                          reads=[wkt, ckv[o_].t[tkk // 4]], writes=[ps.t[0]])
                            evac_copy(Vx.ap[:, t4 * 2:t4 * 2 + 2, :], Vx.t[0], ps.ap[:, :].rearrange("p (s c) -> p s c", c=256), ps.t[0])
                        for hh in range(2):
                            h = hp * 2 + hh
                            pso = PS[3 + hh]
                            psd = PS[5 + hh]
                            pb = 64 * hh

                            def S(kt):
                                c0 = max(0, kt - 4 * tg) * 128
                                ps = next_ps()
                                em.mm(ps.ap[:, c0:TG], [(Kx[hh].ap[:, kt * 128:(kt + 1) * 128], qn[hh].ap[:, c0:TG]),
                                                        (krc[o_].ap[pb:pb + 64, kt * 128:(kt + 1) * 128], qr.ap[pb:pb + 64, c0:TG])],
                                      reads=[Kx[hh].t[0], qn[hh].t[0], krc[o_].t[kt // 4], qr.t[0]], writes=[ps.t[0]])
                                p_ = pTs[pi_[0] % 4]
                                pi_[0] += 1
                                em.op("act", lambda e, ps=ps, p_=p_, c0=c0: e.activation(out=p_.ap[:, c0:TG], in_=ps.ap[:, c0:TG], func=AF.Exp, scale=scale),
                                      reads=[ps.t[0]], writes=[p_.t[0]])
                                if kt >= 4 * tg:
                                    em.op("dve", lambda e, p_=p_, c0=c0: e.memset(p_.ap[64:128, c0:c0 + 64], 0.0), writes=[p_.t[0]])
                                return p_, c0

                            LA = 3
                            pend = [S(k2) for k2 in range(min(LA, nkt))]
                            for kt in range(nkt):
                                if kt + LA < nkt:
                                    pend.append(S(kt + LA))
                                p_, c0 = pend.pop(0)
                                em.mm1(pso.ap[:, c0:TG], Vx.ap[:, kt, hh * 128:(hh + 1) * 128], p_.ap[:, c0:TG], kt == 0, kt == nkt - 1,
                                       reads=[Vx.t[0], p_.t[0]], writes=[pso.t[0]])
                                em.mm1(psd.ap[:, c0:TG], ones.ap[:, :], p_.ap[:, c0:TG], kt == 0, kt == nkt - 1,
                                       reads=[ones.t[0], p_.t[0]], writes=[psd.t[0]])
                            em.op("dve", lambda e, psd=psd: e.reciprocal(out=rden.ap[:, :], in_=psd.ap[:, :]), reads=[psd.t[0]], writes=[rden.t[0]])
                            em.op("dve", lambda e, pso=pso, h=h: e.tensor_tensor(out=oT.ap[:, h, :], in0=pso.ap[:, :], in1=rden.ap[:, :], op=ALU.mult),
                                  reads=[pso.t[0], rden.t[0]], writes=[oT.t[h]])
                    em.barrier()
                    ps_pool[0] = list(range(7))
                proj_to_residual(c_w_out[o_], oT)
                em.barrier()

        def rope_tables(ph, s, tg):
            tok0 = tg * TG
            em.dma("pool", posi.ap[:, :], pos_d[s:s + 1, tok0:tok0 + TG].partition_broadcast(128), s_pb[2], writes=[posi.t[0]])
            a = mkbuf(ph, [128, TG], F32, "ra")
            b = mkbuf(ph, [128, TG], F32, "rb")
            ki = mkbuf(ph, [128, TG], I32, "rki")
            dv = lambda fn, reads, writes: em.op("dve", fn, reads=reads, writes=writes)
            dv(lambda e: e.tensor_copy(out=a.ap[:, :], in_=posi.ap[:, :]), [posi.t[0]], [a.t[0]])
            dv(lambda e: e.tensor_scalar(out=a.ap[:, :], in0=a.ap[:, :], scalar1=ropec.ap[:, 0:1], scalar2=None, op0=ALU.mult), [a.t[0], ropec.t[0]], [a.t[0]])
            dv(lambda e: e.tensor_scalar(out=b.ap[:, :], in0=a.ap[:, :], scalar1=float(1.0 / (2 * np.pi)), scalar2=None, op0=ALU.mult), [a.t[0]], [b.t[0]])
            dv(lambda e: e.tensor_copy(out=ki.ap[:, :], in_=b.ap[:, :]), [b.t[0]], [ki.t[0]])
            dv(lambda e: e.tensor_copy(out=b.ap[:, :], in_=ki.ap[:, :]), [ki.t[0]], [b.t[0]])
            dv(lambda e: e.scalar_tensor_tensor(out=a.ap[:, :], in0=b.ap[:, :], scalar=-6.28125, in1=a.ap[:, :], op0=ALU.mult, op1=ALU.add), [a.t[0], b.t[0]], [a.t[0]])
            dv(lambda e: e.scalar_tensor_tensor(out=a.ap[:, :], in0=b.ap[:, :], scalar=-0.0019353071795864769, in1=a.ap[:, :], op0=ALU.mult, op1=ALU.add),
               [a.t[0], b.t[0]], [a.t[0]])

            def wrap(t_):
                dv(lambda e: e.tensor_scalar(out=b.ap[:, :], in0=t_.ap[:, :], scalar1=PI, scalar2=-2 * PI, op0=ALU.is_gt, op1=ALU.mult), [t_.t[0]], [b.t[0]])
                dv(lambda e: e.tensor_tensor(out=t_.ap[:, :], in0=t_.ap[:, :], in1=b.ap[:, :], op=ALU.add), [t_.t[0], b.t[0]], [t_.t[0]])
                dv(lambda e: e.tensor_scalar(out=b.ap[:, :], in0=t_.ap[:, :], scalar1=-PI, scalar2=2 * PI, op0=ALU.is_lt, op1=ALU.mult), [t_.t[0]], [b.t[0]])
                dv(lambda e: e.tensor_tensor(out=t_.ap[:, :], in0=t_.ap[:, :], in1=b.ap[:, :], op=ALU.add), [t_.t[0], b.t[0]], [t_.t[0]])

            wrap(a)
            em.op("act", lambda e: e.activation(out=St.ap[:, :], in_=a.ap[:, :], func=AF.Sin), reads=[a.t[0]], writes=[St.t[0]])
            dv(lambda e: e.tensor_scalar(out=St.ap[:, :], in0=St.ap[:, :], scalar1=ropec.ap[:, 1:2], scalar2=None, op0=ALU.mult), [St.t[0], ropec.t[0]], [St.t[0]])
            dv(lambda e: e.tensor_scalar(out=a.ap[:, :], in0=a.ap[:, :], scalar1=PI / 2, scalar2=None, op0=ALU.add), [a.t[0]], [a.t[0]])
            wrap(a)
            em.op("act", lambda e: e.activation(out=Ct.ap[:, :], in_=a.ap[:, :], func=AF.Sin), reads=[a.t[0]], writes=[Ct.t[0]])

        def seq_setup(s):
            with ExitStack() as ph:
                ms = [mkbuf(ph, [128, D], F32, f"ms{i}") for i in range(2)]
                junk = mkbuf(ph, [128, D], BF16, "junk")
                for mt in range(2):
                    em.dma("sp", ms[mt].ap[:, :], mem_d[s, mt * 128:(mt + 1) * 128, :], s_in[mt], writes=[ms[mt].t[0]])
                    sm = small.ap[:, mt:mt + 1]
                    em.op("dve", lambda e, sm=sm: e.memset(sm, 0.0), writes=[small.t[0]])
                    em.op("act", lambda e, mt=mt, sm=sm: e.activation(out=junk.ap[:, :], in_=ms[mt].ap[:, :], func=AF.Square, accum_out=sm),
                          reads=[ms[mt].t[0]], writes=[junk.t[0], small.t[0]])
                    em.op("dve", lambda e, sm=sm: e.tensor_scalar(out=sm, in0=sm, scalar1=1.0 / D, scalar2=EPS, op0=ALU.mult, op1=ALU.add), reads=[small.t[0]], writes=[small.t[0]])
                    em.op("act", lambda e, sm=sm: e.activation(out=sm, in_=sm, func=AF.Sqrt), reads=[small.t[0]], writes=[small.t[0]])
                    em.op("dve", lambda e, sm=sm: e.reciprocal(out=sm, in_=sm), reads=[small.t[0]], writes=[small.t[0]])
                    em.op("dve", lambda e, mt=mt, sm=sm: e.tensor_scalar(out=ms[mt].ap[:, :], in0=ms[mt].ap[:, :], scalar1=sm, scalar2=None, op0=ALU.mult),
                          reads=[ms[mt].t[0], small.t[0]], writes=[ms[mt].t[0]])
                    for c4 in range(4):
                        ps = next_ps()
                        for ci in range(4):
                            c = c4 * 4 + ci
                            em.tr(ps.ap[:, ci * 128:(ci + 1) * 128], ms[mt].ap[:, c * 128:(c + 1) * 128], ident.ap[:, :], reads=[ms[mt].t[0], ident.t[0]], writes=[ps.t[0]])
                        evac_copy(memhT.ap[:, c4 * 4:c4 * 4 + 4, mt * 128:(mt + 1) * 128], memhT.t[0], ps.ap[:, :].rearrange("p (c m) -> p c m", m=128), ps.t[0])
                em.barrier()

        def boundary(prev, nxt):
            with ExitStack() as ph:
                if nxt is not None:
                    s2, tg2 = nxt
                    xs = [mkbuf(ph, [128, D], F32, f"xs{i}") for i in range(4)]
                    for tt in range(4):
                        em.dma("sp", xs[tt].ap[:, :], x_d[s2, tg2 * TG + tt * 128: tg2 * TG + (tt + 1) * 128, :], s_in[tt], writes=[xs[tt].t[0]])
                if prev is not None:
                    s1, tg1 = prev
                    os_ = [mkbuf(ph, [128, D], F32, f"os{i}") for i in range(2)]
                    if prenormed[0]:
                        prenormed[0] = False
                        norm_finish(X, NCH, D, G_FINAL, X.t, lambda c: X.ap[:, c, :], PSN)
                    else:
                        rmsnorm(X, NCH, D, G_FINAL, X, X.t, lambda c: X.ap[:, c, :])
                    for tt in range(4):
                        o_ = os_[tt % 2]
                        for c4 in range(4):
                            ps = next_ps()
                            for ci in range(4):
                                c = c4 * 4 + ci
                                em.tr(ps.ap[:, ci * 128:(ci + 1) * 128], X.ap[:, c, tt * 128:(tt + 1) * 128], ident.ap[:, :], reads=[X.t[c], ident.t[0]], writes=[ps.t[0]])
                            evac_copy(o_.ap[:, c4 * 512:(c4 + 1) * 512], o_.t[0], ps.ap[:, :], ps.t[0])
                        em.dma("sp", out_d[s1, tg1 * TG + tt * 128: tg1 * TG + (tt + 1) * 128, :], o_.ap[:, :], s_out[tt % 2], reads=[o_.t[0]])
                if nxt is not None:
                    rope_tables(ph, s2, tg2)
                    for c in range(NCH):
                        ps = next_ps()
                        for tt in range(4):
                            em.tr(ps.ap[:, tt * 128:(tt + 1) * 128], xs[tt].ap[:, c * 128:(c + 1) * 128], ident.ap[:, :], reads=[xs[tt].t[0], ident.t[0]], writes=[ps.t[0]])
                        evac_copy(X.ap[:, c, :], X.t[c], ps.ap[:, :], ps.t[0])
                        x_chunk_final(c)
                    x_all_final()
                em.barrier()

        seq_ph = []
        for l in range(n_layers):
            if "m" in phases:
                seq_ph.append(("m", l, G_MIX + l * 16))
            if "c" in phases:
                seq_ph.append(("c", l, G_MEMQ + l * 16))
            if "f" in phases:
                seq_ph.append(("f", l, G_FFN + l * 16))
        prev = None
        for s in range(n_seq):
            seq_setup(s)
            for tg in range(n_tg):
                nxt_norm[0] = seq_ph[0][2] if seq_ph else None
                boundary(prev, (s, tg))
                for i, (kind, l, g) in enumerate(seq_ph):
                    nxt_norm[0] = seq_ph[i + 1][2] if i + 1 < len(seq_ph) else "final"
                    if kind == "m":
                        if l % 2 == 0:
                            mixer_even(l, tg)
                        else:
                            mixer_odd(l, tg)
                    elif kind == "c":
                        cross(l, tg)
                    else:
                        ffn(l)
                prev = (s, tg)
        nxt_norm[0] = None
        boundary(prev, None)
        for k_ in s_out + [s_dbg]:
            em.streams["sp"].append(("wait", k_, em.cnt[k_]))

        with nc.Block() as block:
            em.replay(block)
    return nc


def make_consts():
    ident = np.eye(128, dtype=np.float32)
    inv = (10000.0 ** (-np.arange(0, 64, 2, dtype=np.float32) / np.float32(64))).astype(np.float32)
    ropec = np.zeros((128, 2), np.float32)
    for p in range(128):
        ropec[p, 0] = inv[p % 32]
        ropec[p, 1] = -1.0 if (p % 64) < 32 else 1.0
    return ident, ropec


def make_gvec(norm_mix_g, norm_mem_q_g, norm_mem_kv_g, norm_ffn_g, final_norm_g, c_q_norm_g, c_kv_norm_g, b_conv_w):
    parts = [norm_mix_g.reshape(-1, 128), norm_mem_q_g.reshape(-1, 128), norm_mem_kv_g.reshape(-1, 128), norm_ffn_g.reshape(-1, 128),
             final_norm_g.reshape(-1, 128), c_q_norm_g.reshape(-1, 128), c_kv_norm_g.reshape(-1, 128), b_conv_w.reshape(-1, 128)]
    g = np.ascontiguousarray(np.concatenate(parts, axis=0).astype(np.float32))
    assert g.shape == (G_ROWS, 128)
    return g


_NC_CACHE = {}


def kernel(x, mem, positions, norm_mix_g, norm_mem_q_g, norm_mem_kv_g, norm_ffn_g, final_norm_g, ab_w_in, a_v_norm_g, a_w_s, a_b_s,
           b_conv_w, ab_w_out, c_w_in, c_q_norm_g, c_kv_norm_g, c_w_uq, c_w_ukv, c_w_out, m_wq, m_wk, m_wv, m_wo, f_w1, f_w2):
    n_cores = 8
    f = lambda a: np.ascontiguousarray(np.asarray(a, dtype=np.float32))
    ident, ropec = make_consts()
    gvec = make_gvec(f(norm_mix_g), f(norm_mem_q_g), f(norm_mem_kv_g), f(norm_ffn_g), f(final_norm_g), f(c_q_norm_g), f(c_kv_norm_g), f(b_conv_w))
    shared = {
        "gvec": gvec, "ident": ident, "ropec": ropec,
        "ab_w_in": f(ab_w_in), "a_v_norm_g": f(a_v_norm_g), "a_w_s": f(a_w_s), "a_b_s": f(a_b_s).reshape(2, 1024),
        "ab_w_out": f(ab_w_out), "c_w_in": f(c_w_in), "c_w_uq": f(c_w_uq), "c_w_ukv": f(c_w_ukv), "c_w_out": f(c_w_out),
        "m_wq": f(m_wq), "m_wk": f(m_wk), "m_wv": f(m_wv), "m_wo": f(m_wo), "f_w1": f(f_w1), "f_w2": f(f_w2),
    }
    x = f(x)
    mem = f(mem)
    positions = np.ascontiguousarray(np.asarray(positions, dtype=np.int32))
    if "nc" not in _NC_CACHE:
        _NC_CACHE["nc"] = build_program()
    nc = _NC_CACHE["nc"]
    in_maps = []
    for c in range(n_cores):
        m = dict(shared)
        m["x"] = np.ascontiguousarray(x[2 * c:2 * c + 2])
        m["mem"] = np.ascontiguousarray(mem[2 * c:2 * c + 2])
        m["positions"] = np.ascontiguousarray(positions[2 * c:2 * c + 2])
        in_maps.append(m)
    res = run_bass_kernel_spmd(nc, in_maps, core_ids=list(range(n_cores)))
    return np.concatenate([r["out"] for r in res.results], axis=0).astype(np.float32)
```

```python
from contextlib import ExitStack

import numpy as np
import concourse.bass as bass
import concourse.mybir as mybir
from concourse.bass_utils import run_bass_kernel_spmd

F32 = mybir.dt.float32
BF16 = mybir.dt.bfloat16
I32 = mybir.dt.int32
AF = mybir.ActivationFunctionType
ALU = mybir.AluOpType
AX = mybir.AxisListType

D = 2048
NCH = 16
TG = 512
SEQ = 2048
NMEM = 256
DFF = 8192
EPS = 1e-6
PI = float(np.pi)
NSLOT = 5

G_MIX, G_MEMQ, G_MEMKV, G_FFN, G_FINAL = 0, 64, 128, 192, 256
G_CQ, G_CKV, G_CONV = 272, 280, 284
G_ROWS = 332


class T:
    __slots__ = ("name", "w", "r")

    def __init__(self, name):
        self.name = name
        self.w = None
        self.r = {}


class Buf:
    def __init__(self, ap, tiles):
        self.ap = ap
        self.t = tiles


class Em:
    ENGS = ("pe", "act", "dve", "pool", "sp")

    def __init__(self, nc, es):
        self.nc = nc
        self.es = es
        self.streams = {e: [] for e in self.ENGS}
        self.sem = {}
        self.cnt = {}
        self.waited = {e: {} for e in self.ENGS}
        self.barrier_dma = set()
        for e in self.ENGS:
            self.newsem("eng_" + e)

    def newsem(self, key):
        self.sem[key] = self.es.enter_context(self.nc.semaphore(key))
        self.cnt[key] = 0
        return key

    def _deps(self, reads, writes):
        deps = []
        for t in reads:
            if t.w is not None:
                deps.append(t.w)
        for t in writes:
            if t.w is not None:
                deps.append(t.w)
            deps.extend(t.r.items())
        return deps

    def _wait(self, eng, deps):
        own = "eng_" + eng
        wd = self.waited[eng]
        need = {}
        for k, v in deps:
            if k == own and eng == "pe":
                continue
            if wd.get(k, 0) >= v:
                continue
            if need.get(k, 0) < v:
                need[k] = v
        for k, v in need.items():
            self.streams[eng].append(("wait", k, v))
            wd[k] = v

    def _mark(self, ev, reads, writes):
        for t in reads:
            if t.r.get(ev[0], 0) < ev[1]:
                t.r[ev[0]] = ev[1]
        for t in writes:
            t.w = ev
            t.r = {}

    def op(self, eng, fn, reads=(), writes=()):
        self._wait(eng, self._deps(reads, writes))
        k = "eng_" + eng
        self.cnt[k] += 1
        ev = (k, self.cnt[k])
        self.streams[eng].append(("op", fn, k, 1))
        self._mark(ev, reads, writes)
        return ev

    def mm(self, out_ap, pairs, reads, writes, pair_reads=None):
        self._wait("pe", self._deps(reads, writes))
        k = "eng_pe"
        n = len(pairs)
        allreads = list(reads)
        for i, (l, r) in enumerate(pairs):
            if pair_reads is not None:
                self._wait("pe", self._deps(pair_reads[i], ()))
                allreads.extend(pair_reads[i])
            last = i == n - 1
            fn = (lambda e, l=l, r=r, i=i, last=last: e.matmul(out_ap, lhsT=l, rhs=r, start=(i == 0), stop=last))
            if last:
                self.cnt[k] += 1
                self.streams["pe"].append(("op", fn, k, 1))
            else:
                self.streams["pe"].append(("op", fn, None, 0))
        ev = (k, self.cnt[k])
        self._mark(ev, allreads, writes)
        return ev

    def mm1(self, out_ap, l, r, start, stop, reads, writes):
        self._wait("pe", self._deps(reads, writes))
        k = "eng_pe"
        self.cnt[k] += 1
        self.streams["pe"].append(("op", lambda e: e.matmul(out_ap, lhsT=l, rhs=r, start=start, stop=stop), k, 1))
        ev = (k, self.cnt[k])
        self._mark(ev, reads, writes)
        return ev

    def tr(self, out_ap, in_ap, ident_ap, reads, writes):
        self._wait("pe", self._deps(reads, writes))
        k = "eng_pe"
        self.cnt[k] += 1
        self.streams["pe"].append(("op", lambda e: e.transpose(out_ap, in_ap, ident_ap), k, 1))
        ev = (k, self.cnt[k])
        self._mark(ev, reads, writes)
        return ev

    def dma(self, q, out_ap, in_ap, semkey, reads=(), writes=(), barrier=True, skip_own=False):
        deps = self._deps(reads, writes)
        if skip_own:
            deps = [d for d in deps if d[0] != semkey]
        self._wait(q, deps)
        self.cnt[semkey] += 16
        ev = (semkey, self.cnt[semkey])
        self.streams[q].append(("op", lambda e: e.dma_start(out=out_ap, in_=in_ap), semkey, 16))
        self._mark(ev, reads, writes)
        if barrier:
            self.barrier_dma.add(semkey)
        return ev

    def barrier(self, engs=("pe", "act", "dve", "sp")):
        evs = [("eng_" + e, self.cnt["eng_" + e]) for e in ("pe", "act", "dve")]
        evs += [(k, self.cnt[k]) for k in self.barrier_dma]
        for e in engs:
            self._wait(e, evs)

    def replay(self, block):
        em = self

        def run(handle, stream):
            for it in stream:
                if it[0] == "wait":
                    handle.wait_ge(em.sem[it[1]], it[2])
                else:
                    ins = it[1](handle)
                    if it[2] is not None:
                        ins.then_inc(em.sem[it[2]], it[3])

        @block.tensor
        def _(t):
            run(t, em.streams["pe"])

        @block.scalar
        def _(a):
            run(a, em.streams["act"])

        @block.vector
        def _(v):
            run(v, em.streams["dve"])

        @block.gpsimd
        def _(g):
            run(g, em.streams["pool"])

        @block.sync
        def _(s):
            run(s, em.streams["sp"])


def build_program(n_seq=2, n_tg=4, n_layers=4, phases="mcf", dbg=None):
    nc = bass.Bass("TRN2", target_bir_lowering=False)
    dt = lambda name, shape, d=F32, kind="ExternalInput": nc.dram_tensor(name, list(shape), d, kind=kind).ap()
    x_d = dt("x", [n_seq, SEQ, D])
    mem_d = dt("mem", [n_seq, NMEM, D])
    pos_d = dt("positions", [n_seq, SEQ], I32)
    gvec_d = dt("gvec", [G_ROWS, 128])
    ident_d = dt("ident", [128, 128])
    ropec_d = dt("ropec", [128, 2])
    ab_w_in = dt("ab_w_in", [2, D, 5120])
    a_v_norm_g = dt("a_v_norm_g", [2, 1024])
    a_w_s = dt("a_w_s", [2, 8, 128, 128])
    a_b_s = dt("a_b_s", [2, 1024])
    ab_w_out = dt("ab_w_out", [2, D, D])
    c_w_in = dt("c_w_in", [2, D, 832])
    c_w_uq = dt("c_w_uq", [2, 512, 3072])
    c_w_ukv = dt("c_w_ukv", [2, 256, 4096])
    c_w_out = dt("c_w_out", [2, D, D])
    m_wq = dt("m_wq", [4, D, D])
    m_wk = dt("m_wk", [4, D, D])
    m_wv = dt("m_wv", [4, D, D])
    m_wo = dt("m_wo", [4, D, D])
    f_w1 = dt("f_w1", [4, D, DFF])
    f_w2 = dt("f_w2", [4, DFF, D])
    out_d = dt("out", [n_seq, SEQ, D], F32, "ExternalOutput")
    dbg_d = dt("dbg", [16, 128, 512], F32, "ExternalOutput") if dbg else None
    ascr = [dt(f"ascr{l}", [4, 128, NCH * NMEM], BF16, "Internal") for l in range(4)]
    bscr = [dt(f"bscr{l}", [4, 128, 2 * D], BF16, "Internal") for l in range(4)]

    with ExitStack() as es:
        em = Em(nc, es)
        ctr = [0]

        def sb(es_, shape, d, name=None):
            ctr[0] += 1
            return es_.enter_context(nc.sbuf_tensor(f"{name or 't'}_{ctr[0]}", list(shape), d))

        def mkbuf(es_, shape, d, name, ntiles=None):
            t = sb(es_, shape, d, name)
            n = ntiles if ntiles is not None else 1
            return Buf(t, [T(f"{name}{i}") for i in range(n)])

        X = mkbuf(es, [128, NCH, TG], F32, "X", NCH)
        xn = mkbuf(es, [128, NCH, TG], BF16, "xn", NCH)
        Wr = [mkbuf(es, [128, 4096], BF16, f"W{i}") for i in range(NSLOT)]
        wsem = [em.newsem(f"wsem{i}") for i in range(NSLOT)]
        ckv = [mkbuf(es, [128, 2, SEQ], BF16, f"ckv{i}", 4) for i in range(2)]
        krc = [mkbuf(es, [128, SEQ], BF16, f"krc{i}", 4) for i in range(2)]
        memhT = mkbuf(es, [128, NCH, NMEM], BF16, "memhT")
        gT = mkbuf(es, [128, G_ROWS], F32, "gT")
        ident = mkbuf(es, [128, 128], F32, "ident")
        ones = mkbuf(es, [128, 128], BF16, "ones")
        identb = mkbuf(es, [128, 128], BF16, "identb")
        ropec = mkbuf(es, [128, 2], F32, "ropec")
        wsT = [mkbuf(es, [128, 8, 128], BF16, f"wsT{i}") for i in range(2)]
        Ct = mkbuf(es, [128, TG], F32, "Ct")
        St = mkbuf(es, [128, TG], F32, "St")
        rstd = mkbuf(es, [128, TG], F32, "rstd")
        r1 = mkbuf(es, [128, TG], F32, "r1")
        sq = [mkbuf(es, [128, TG], BF16, f"sq{i}") for i in range(4)]
        halo = [mkbuf(es, [128, 8, 2], F32, f"halo{i}") for i in range(2)]
        posi = mkbuf(es, [128, TG], I32, "posi")
        bsb = mkbuf(es, [128, 1024], F32, "bsb")
        vng = mkbuf(es, [128, 1024], F32, "vng")
        small = mkbuf(es, [128, 8], F32, "small")
        PS = [Buf(es.enter_context(nc.psum_tensor(f"ps{i}", [128, 512], F32)), [T(f"ps{i}")]) for i in range(8)]
        psi = [0]

        ps_pool = [list(range(7))]
        PSN = PS[7]

        def next_ps():
            pool_ = ps_pool[0]
            p = PS[pool_[psi[0] % len(pool_)]]
            psi[0] += 1
            return p

        s_in = [em.newsem(f"s_in{i}") for i in range(4)]
        s_out = [em.newsem(f"s_out{i}") for i in range(2)]
        s_misc = [em.newsem(f"s_misc{i}") for i in range(5)]
        s_pb = [em.newsem(f"s_pb{i}") for i in range(3)]
        wj = [0]
        evac_rr = [0]
        ascrT = [T(f"ascr{l}") for l in range(4)]
        bscrT = [T(f"bscr{l}") for l in range(4)]
        s_aw = [em.newsem(f"s_aw{l}") for l in range(4)]
        s_bw = [em.newsem(f"s_bw{l}") for l in range(4)]

        s_dbg = em.newsem("s_dbg")
        dbgt = mkbuf(es, [128, TG], F32, "dbgt") if dbg else None

        def dump(name, ap_of_chunk, tiles, n):
            if dbg != name:
                return
            for c in range(n):
                em.op("dve", lambda e, c=c: e.tensor_copy(out=dbgt.ap[:, :], in_=ap_of_chunk(c)), reads=list(tiles), writes=[dbgt.t[0]])
                em.dma("sp", dbg_d[c, :, :], dbgt.ap[:, :], s_dbg, reads=[dbgt.t[0]])

        def gcol(base, i=0):
            return gT.ap[:, base + i:base + i + 1]

        def evac_copy(out_ap, out_t, ps_ap, ps_t, eng=None):
            if eng is None:
                eng = "act" if evac_rr[0] % 2 == 0 else "dve"
                evac_rr[0] += 1
            if eng == "act":
                em.op("act", lambda e: e.copy(out=out_ap, in_=ps_ap), reads=[ps_t], writes=[out_t])
            else:
                em.op("dve", lambda e: e.tensor_copy(out=out_ap, in_=ps_ap), reads=[ps_t], writes=[out_t])

        def wslot():
            s = wj[0] % NSLOT
            wj[0] += 1
            return s

        def wblock_k2048(w2d, row0, col0, ncols=256):
            s = wslot()
            v = Wr[s].ap[:, :].rearrange("p (k f) -> p k f", f=256)
            for kq in range(4):
                src = w2d[row0 + kq * 512: row0 + (kq + 1) * 512, col0:col0 + ncols].rearrange("(k p) f -> p k f", p=128)
                em.dma("pool", v[:, kq * 4:(kq + 1) * 4, 0:ncols], src, wsem[s], writes=[Wr[s].t[0]], barrier=False, skip_own=True)
            return v, Wr[s].t[0]

        def wblock_generic(w2d, nk, col0, ncols):
            s = wslot()
            v = Wr[s].ap[:, 0:nk * ncols].rearrange("p (k f) -> p k f", f=ncols)
            src = w2d[:, col0:col0 + ncols].rearrange("(k p) f -> p k f", p=128)
            em.dma("pool", v, src, wsem[s], writes=[Wr[s].t[0]], barrier=False)
            return v, Wr[s].t[0]

        def norm_stats_chunk(src, c, nch, ps):
            s_ = sq[c % 4]
            st = src.t[c] if len(src.t) > 1 else src.t[0]
            em.op("act", lambda e, c=c, s_=s_: e.activation(out=s_.ap[:, :], in_=src.ap[:, c, :], func=AF.Square), reads=[st], writes=[s_.t[0]])
            em.mm1(ps.ap[:, :], ones.ap[:, :], s_.ap[:, :], c == 0, c == nch - 1, reads=[s_.t[0], ones.t[0]], writes=[ps.t[0]])

        def norm_finish(src, nch, dim, gbase, dst_tiles, dst_slicer, ps):
            em.op("act", lambda e: e.activation(out=r1.ap[:, :], in_=ps.ap[:, :], func=AF.Ln, scale=1.0 / dim, bias=EPS), reads=[ps.t[0]], writes=[r1.t[0]])
            em.op("act", lambda e: e.activation(out=rstd.ap[:, :], in_=r1.ap[:, :], func=AF.Exp, scale=-0.5), reads=[r1.t[0]], writes=[rstd.t[0]])
            for c in range(nch):
                em.op("dve", lambda e, c=c: e.scalar_tensor_tensor(out=dst_slicer(c), in0=src.ap[:, c, :], scalar=gcol(gbase, c), in1=rstd.ap[:, :],
                                                                   op0=ALU.mult, op1=ALU.mult),
                      reads=[src.t[c] if len(src.t) > 1 else src.t[0], rstd.t[0], gT.t[0]], writes=[dst_tiles[c] if len(dst_tiles) > 1 else dst_tiles[0]])

        def rmsnorm(src, nch, dim, gbase, dst, dst_tiles, dst_slicer):
            ps = next_ps()
            for c in range(nch):
                norm_stats_chunk(src, c, nch, ps)
            norm_finish(src, nch, dim, gbase, dst_tiles, dst_slicer, ps)

        nxt_norm = [None]
        prenormed = [False]

        pend_stats = []
        STATS_LAG = 3

        def x_chunk_final(dc):
            if nxt_norm[0] is not None:
                pend_stats.append(dc)
                while len(pend_stats) > STATS_LAG:
                    norm_stats_chunk(X, pend_stats.pop(0), NCH, PSN)

        def x_all_final():
            if nxt_norm[0] is None:
                return
            while pend_stats:
                norm_stats_chunk(X, pend_stats.pop(0), NCH, PSN)
            prenormed[0] = True

        def norm_x(gbase):
            if prenormed[0]:
                prenormed[0] = False
                norm_finish(X, NCH, D, gbase, xn.t, lambda c: xn.ap[:, c, :], PSN)
                return
            rmsnorm(X, NCH, D, gbase, xn, xn.t, lambda c: xn.ap[:, c, :])

        def proj_to_residual(w2d, src, nblk=8):
            for blk in range(nblk):
                wv, wt = wblock_k2048(w2d, 0, blk * 256)
                for j in range(2):
                    dc = blk * 2 + j
                    ps = next_ps()
                    em.mm(ps.ap[:, :], [(wv[:, kc, j * 128:(j + 1) * 128], src.ap[:, kc, :]) for kc in range(NCH)],
                          reads=[wt], pair_reads=[[src.t[kc]] for kc in range(NCH)], writes=[ps.t[0]])
                    em.op("dve", lambda e, dc=dc, ps=ps: e.tensor_tensor(out=X.ap[:, dc, :], in0=ps.ap[:, :], in1=X.ap[:, dc, :], op=ALU.add),
                          reads=[ps.t[0], X.t[dc]], writes=[X.t[dc]])
                    x_chunk_final(dc)
            x_all_final()

        em.dma("sp", ident.ap[:, :], ident_d[:, :], s_misc[3], writes=[ident.t[0]])
        em.dma("sp", ropec.ap[:, :], ropec_d[:, :], s_misc[4], writes=[ropec.t[0]])
        em.op("dve", lambda e: e.memset(ones.ap[:, :], 1.0), writes=[ones.t[0]])
        em.op("dve", lambda e: e.tensor_copy(out=identb.ap[:, :], in_=ident.ap[:, :]), reads=[ident.t[0]], writes=[identb.t[0]])
        with ExitStack() as ph:
            gst = [mkbuf(ph, [128, 128], F32, f"gst{i}") for i in range(3)]
            r0 = 0
            for i in range(3):
                nr = min(128, G_ROWS - r0)
                if nr < 128:
                    em.op("dve", lambda e, i=i: e.memset(gst[i].ap[:, :], 0.0), writes=[gst[i].t[0]])
                em.dma("sp", gst[i].ap[0:nr, :], gvec_d[r0:r0 + nr, :], s_misc[i], writes=[gst[i].t[0]])
                ps = next_ps()
                em.tr(ps.ap[:, 0:128], gst[i].ap[:, :], ident.ap[:, :], reads=[gst[i].t[0], ident.t[0]], writes=[ps.t[0]])
                em.op("dve", lambda e, ps=ps, r0=r0, nr=nr: e.tensor_copy(out=gT.ap[:, r0:r0 + nr], in_=ps.ap[:, 0:nr]), reads=[ps.t[0]], writes=[gT.t[0]])
                r0 += nr
            for e_ in range(2):
                for g in range(8):
                    st_ = gst[g % 3]
                    em.dma("sp", st_.ap[:, :], a_w_s[e_, g, :, :], s_misc[g % 3], writes=[st_.t[0]])
                    ps = next_ps()
                    em.tr(ps.ap[:, 0:128], st_.ap[:, :], ident.ap[:, :], reads=[st_.t[0], ident.t[0]], writes=[ps.t[0]])
                    em.op("act", lambda e, ps=ps, e_=e_, g=g: e.copy(out=wsT[e_].ap[:, g, :], in_=ps.ap[:, 0:128]), reads=[ps.t[0]], writes=[wsT[e_].t[0]])
                em.op("dve", lambda e, e_=e_: e.memset(wsT[e_].ap[64:128, :, 0:64], 0.0), writes=[wsT[e_].t[0]])
            em.barrier()

        def ffn(l):
            with ExitStack() as ph:
                hT = [mkbuf(ph, [128, NCH, TG], BF16, f"hT{i}", NCH) for i in range(2)]
                rs = [mkbuf(ph, [128, TG], F32, f"rs{i}") for i in range(2)]
                norm_x(G_FFN + l * 16)
                k_ = [0]

                def w1(q):
                    h = hT[q % 2]
                    for fb in range(8):
                        wv, wt = wblock_k2048(f_w1[l], 0, q * 2048 + fb * 256)
                        for j in range(2):
                            fc = fb * 2 + j
                            ps = next_ps()
                            em.mm(ps.ap[:, :], [(wv[:, kc, j * 128:(j + 1) * 128], xn.ap[:, kc, :]) for kc in range(NCH)],
                                  reads=[wt], pair_reads=[[xn.t[kc]] for kc in range(NCH)], writes=[ps.t[0]])
                            r = rs[k_[0] % 2]
                            k_[0] += 1
                            em.op("act", lambda e, ps=ps, r=r: e.activation(out=r.ap[:, :], in_=ps.ap[:, :], func=AF.Relu), reads=[ps.t[0]], writes=[r.t[0]])
                            em.op("dve", lambda e, r=r, h=h, fc=fc: e.tensor_tensor(out=h.ap[:, fc, :], in0=r.ap[:, :], in1=r.ap[:, :], op=ALU.mult),
                                  reads=[r.t[0]], writes=[h.t[fc]])

                def w2(q):
                    h = hT[q % 2]
                    for db in range(8):
                        wv, wt = wblock_k2048(f_w2[l], q * 2048, db * 256)
                        for j in range(2):
                            dc = db * 2 + j
                            ps = next_ps()
                            em.mm(ps.ap[:, :], [(wv[:, kc, j * 128:(j + 1) * 128], h.ap[:, kc, :]) for kc in range(NCH)],
                                  reads=[wt], pair_reads=[[h.t[kc]] for kc in range(NCH)], writes=[ps.t[0]])
                            em.op("dve", lambda e, dc=dc, ps=ps: e.tensor_tensor(out=X.ap[:, dc, :], in0=ps.ap[:, :], in1=X.ap[:, dc, :], op=ALU.add),
                                  reads=[ps.t[0], X.t[dc]], writes=[X.t[dc]])
                            if q == 3:
                                x_chunk_final(dc)

                w1(0)
                for q in range(4):
                    if q + 1 < 4:
                        w1(q + 1)
                    w2(q)
                x_all_final()
                em.barrier()

        def cross(l, tg):
            if tg == 0:
                with ExitStack() as ph:
                    memn = mkbuf(ph, [128, NCH, NMEM], BF16, "memn")
                    kT = mkbuf(ph, [128, NCH, NMEM], BF16, "kT")
                    vT = mkbuf(ph, [128, NCH, NMEM], BF16, "vT")
                    wqT = mkbuf(ph, [128, 4, D], BF16, "wqT")
                    Asb = mkbuf(ph, [128, NCH, NMEM], BF16, "Asb")
                    Bsb = mkbuf(ph, [128, 2, D], BF16, "Bsb")
                    for c in range(NCH):
                        em.op("dve", lambda e, c=c: e.tensor_scalar(out=memn.ap[:, c, :], in0=memhT.ap[:, c, :], scalar1=gcol(G_MEMKV + l * 16, c), scalar2=None, op0=ALU.mult),
                              reads=[memhT.t[0], gT.t[0]], writes=[memn.t[0]])
                    for (wmat, dstb) in ((m_wk[l], kT), (m_wv[l], vT)):
                        for blk in range(8):
                            wv, wt = wblock_k2048(wmat, 0, blk * 256)
                            for j in range(2):
                                fc = blk * 2 + j
                                ps = next_ps()
                                em.mm(ps.ap[:, 0:NMEM], [(wv[:, kc, j * 128:(j + 1) * 128], memn.ap[:, kc, :]) for kc in range(NCH)],
                                      reads=[wt, memn.t[0]], writes=[ps.t[0]])
                                evac_copy(dstb.ap[:, fc, :], dstb.t[0], ps.ap[:, 0:NMEM], ps.t[0])
                    for h in range(4):
                        for b2 in range(2):
                            wv, wt = wblock_k2048(m_wq[l], 0, (h * 2 + b2) * 256)
                            for j in range(2):
                                jc = b2 * 2 + j
                                for d4 in range(4):
                                    ps = next_ps()
                                    for di in range(4):
                                        dc = d4 * 4 + di
                                        em.mm1(ps.ap[:, di * 128:(di + 1) * 128], wv[:, dc, j * 128:(j + 1) * 128], identb.ap[:, :], True, True,
                                               reads=[wt, identb.t[0]], writes=[ps.t[0]])
                                    evac_copy(wqT.ap[:, jc, d4 * 512:(d4 + 1) * 512], wqT.t[0], ps.ap[:, :], ps.t[0])
                        for d2 in range(8):
                            ps = next_ps()
                            for di in range(2):
                                dc = d2 * 2 + di
                                em.mm(ps.ap[:, di * 256:(di + 1) * 256], [(wqT.ap[:, jc, dc * 128:(dc + 1) * 128], kT.ap[:, 4 * h + jc, :]) for jc in range(4)],
                                      reads=[wqT.t[0], kT.t[0]], writes=[ps.t[0]])
                            evac_copy(Asb.ap[:, d2 * 2:d2 * 2 + 2, :], Asb.t[0], ps.ap[:, :].rearrange("p (c m) -> p c m", m=256), ps.t[0])
                        em.dma("sp", ascr[l][h, :, :], Asb.ap[:, :, :].rearrange("p c m -> p (c m)"), s_aw[l], reads=[Asb.t[0]], writes=[ascrT[l]])
                        for half in range(2):
                            s_ = wslot()
                            wv = Wr[s_].ap[:, :].rearrange("p (k f) -> p k f", f=1024)
                            src = m_wo[l][h * 512:(h + 1) * 512, half * 1024:(half + 1) * 1024].rearrange("(k p) f -> p k f", p=128)
                            em.dma("pool", wv, src, wsem[s_], writes=[Wr[s_].t[0]], barrier=False, skip_own=True)
                            wt = Wr[s_].t[0]
                            for mt in range(2):
                                for dg in range(2):
                                    ps = next_ps()
                                    em.mm(ps.ap[:, :], [(vT.ap[:, 4 * h + jc, mt * 128:(mt + 1) * 128], wv[:, jc, dg * 512:(dg + 1) * 512]) for jc in range(4)],
                                          reads=[wt, vT.t[0]], writes=[ps.t[0]])
                                    d0 = half * 1024 + dg * 512
                                    evac_copy(Bsb.ap[:, mt, d0:d0 + 512], Bsb.t[0], ps.ap[:, :], ps.t[0])
                        em.dma("sp", bscr[l][h, :, :], Bsb.ap[:, :, :].rearrange("p t f -> p (t f)"), s_bw[l], reads=[Bsb.t[0]], writes=[bscrT[l]])
                    em.barrier()
            with ExitStack() as ph:
                pT = [mkbuf(ph, [128, 2, TG], BF16, f"pT{i}") for i in range(2)]
                pn = mkbuf(ph, [128, 8, TG], BF16, "pn", 8)
                rden = mkbuf(ph, [128, TG], F32, "rden")
                norm_x(G_MEMQ + l * 16)
                scale = float(512 ** -0.5)
                for h in range(4):
                    s_ = wslot()
                    av = Wr[s_].ap[:, :].rearrange("p (c m) -> p c m", m=256)
                    em.dma("pool", Wr[s_].ap[:, :], ascr[l][h, :, :], wsem[s_], reads=[ascrT[l]], writes=[Wr[s_].t[0]], barrier=False, skip_own=True)
                    wt = Wr[s_].t[0]
                    p_ = pT[h % 2]
                    for mt in range(2):
                        ps = next_ps()
                        em.mm(ps.ap[:, :], [(av[:, dc, mt * 128:(mt + 1) * 128], xn.ap[:, dc, :]) for dc in range(NCH)],
                              reads=[wt], pair_reads=[[xn.t[dc]] for dc in range(NCH)], writes=[ps.t[0]])
                        em.op("act", lambda e, ps=ps, p_=p_, mt=mt: e.activation(out=p_.ap[:, mt, :], in_=ps.ap[:, :], func=AF.Exp, scale=scale),
                              reads=[ps.t[0]], writes=[p_.t[0]])
                    psd = next_ps()
                    em.mm(psd.ap[:, :], [(ones.ap[:, :], p_.ap[:, mt, :]) for mt in range(2)], reads=[ones.t[0], p_.t[0]], writes=[psd.t[0]])
                    em.op("dve", lambda e, psd=psd: e.reciprocal(out=rden.ap[:, :], in_=psd.ap[:, :]), reads=[psd.t[0]], writes=[rden.t[0]])
                    for mt in range(2):
                        em.op("dve", lambda e, p_=p_, mt=mt, h=h: e.tensor_tensor(out=pn.ap[:, 2 * h + mt, :], in0=p_.ap[:, mt, :], in1=rden.ap[:, :], op=ALU.mult),
                              reads=[p_.t[0], rden.t[0]], writes=[pn.t[2 * h + mt]])
                bview = bscr[l][:, :, :].rearrange("h p (t f) -> p h t f", t=2)
                for dblk in range(8):
                    s_ = wslot()
                    wv = Wr[s_].ap[:, 0:8 * 256].rearrange("p (k f) -> p k f", f=256)
                    wv4 = Wr[s_].ap[:, 0:8 * 256].rearrange("p (h t f) -> p h t f", t=2, f=256)
                    for h4 in range(4):
                        em.dma("pool", wv4[:, h4, :, :], bview[:, h4, :, dblk * 256:(dblk + 1) * 256], wsem[s_],
                               reads=[bscrT[l]], writes=[Wr[s_].t[0]], barrier=False, skip_own=True)
                    wt = Wr[s_].t[0]
                    for j in range(2):
                        dc = dblk * 2 + j
                        ps = next_ps()
                        em.mm(ps.ap[:, :], [(wv[:, k8, j * 128:(j + 1) * 128], pn.ap[:, k8, :]) for k8 in range(8)],
                              reads=[wt], pair_reads=[[pn.t[k8]] for k8 in range(8)], writes=[ps.t[0]])
                        em.op("dve", lambda e, dc=dc, ps=ps: e.tensor_tensor(out=X.ap[:, dc, :], in0=ps.ap[:, :], in1=X.ap[:, dc, :], op=ALU.add),
                              reads=[ps.t[0], X.t[dc]], writes=[X.t[dc]])
                        x_chunk_final(dc)
                x_all_final()
                em.barrier()

        def mixer_even(l, tg):
            e_ = l // 2
            W = ab_w_in[e_]
            with ExitStack() as ph:
                Bz = mkbuf(ph, [128, 8, TG + 2], F32, "Bz", 8)
                vn = mkbuf(ph, [128, 4, 1024], BF16, "vn", 4)
                y = mkbuf(ph, [128, NCH, TG], BF16, "y", NCH)
                vs = [mkbuf(ph, [128, 256], F32, f"vs{i}") for i in range(2)]
                vq = [mkbuf(ph, [128, 256], F32, f"vq{i}") for i in range(2)]
                gu = [mkbuf(ph, [128, TG], F32, f"gu{i}") for i in range(2)]
                tm = [mkbuf(ph, [128, TG], F32, f"tm{i}") for i in range(2)]
                ss = [mkbuf(ph, [128, 2], F32, f"ss{i}") for i in range(2)]
                em.dma("pool", bsb.ap[:, :], a_b_s[e_:e_ + 1, :].partition_broadcast(128), s_pb[0], writes=[bsb.t[0]])
                em.dma("pool", vng.ap[:, :], a_v_norm_g[e_:e_ + 1, :].partition_broadcast(128), s_pb[1], writes=[vng.t[0]])
                norm_x(G_MIX + l * 16)
                if tg == 0:
                    em.op("dve", lambda e: e.memset(Bz.ap[:, :, 0:2], 0.0), writes=Bz.t)
                else:
                    em.op("dve", lambda e: e.tensor_copy(out=Bz.ap[:, :, 0:2], in_=halo[e_].ap[:, :, :]), reads=[halo[e_].t[0]], writes=Bz.t)
                k_ = 0
                for vb in range(4):
                    wv, wt = wblock_k2048(W, 0, 1024 + vb * 256)
                    for tt in range(4):
                        ps = next_ps()
                        em.mm(ps.ap[:, 0:256], [(xn.ap[:, kc, tt * 128:(tt + 1) * 128], wv[:, kc, :]) for kc in range(NCH)],
                              reads=[wt], pair_reads=[[xn.t[kc]] for kc in range(NCH)], writes=[ps.t[0]])
                        v_ = vs[k_ % 2]
                        q_ = vq[k_ % 2]
                        s_ = ss[k_ % 2]
                        k_ += 1
                        em.op("act", lambda e, ps=ps, v_=v_: e.activation(out=v_.ap[:, :], in_=ps.ap[:, 0:256], func=AF.Gelu_apprx_tanh), reads=[ps.t[0]], writes=[v_.t[0]])
                        em.op("dve", lambda e, v_=v_, q_=q_: e.tensor_tensor(out=q_.ap[:, :], in0=v_.ap[:, :], in1=v_.ap[:, :], op=ALU.mult), reads=[v_.t[0]], writes=[q_.t[0]])
                        em.op("dve", lambda e, q_=q_, s_=s_: e.tensor_reduce(out=s_.ap[:, :], in_=q_.ap[:, :].rearrange("p (g c) -> p g c", c=128), axis=AX.X, op=ALU.add),
                              reads=[q_.t[0]], writes=[s_.t[0]])
                        em.op("dve", lambda e, s_=s_: e.tensor_scalar(out=s_.ap[:, :], in0=s_.ap[:, :], scalar1=1.0 / 128, scalar2=EPS, op0=ALU.mult, op1=ALU.add),
                              reads=[s_.t[0]], writes=[s_.t[0]])
                        em.op("act", lambda e, s_=s_: e.activation(out=s_.ap[:, :], in_=s_.ap[:, :], func=AF.Sqrt), reads=[s_.t[0]], writes=[s_.t[0]])
                        em.op("dve", lambda e, s_=s_: e.reciprocal(out=s_.ap[:, :], in_=s_.ap[:, :]), reads=[s_.t[0]], writes=[s_.t[0]])
                        for g2 in range(2):
                            c0 = vb * 256 + g2 * 128
                            em.op("dve", lambda e, v_=v_, s_=s_, g2=g2, c0=c0, tt=tt: e.scalar_tensor_tensor(
                                out=vn.ap[:, tt, c0:c0 + 128], in0=v_.ap[:, g2 * 128:(g2 + 1) * 128], scalar=s_.ap[:, g2:g2 + 1], in1=vng.ap[:, c0:c0 + 128],
                                op0=ALU.mult, op1=ALU.mult), reads=[v_.t[0], s_.t[0], vng.t[0]], writes=[vn.t[tt]])
                k_ = 0
                for ub in range(4):
                    wv, wt = wblock_k2048(W, 0, ub * 256)
                    for j in range(2):
                        g = ub * 2 + j
                        psu = next_ps()
                        em.mm(psu.ap[:, :], [(wv[:, kc, j * 128:(j + 1) * 128], xn.ap[:, kc, :]) for kc in range(NCH)],
                              reads=[wt], pair_reads=[[xn.t[kc]] for kc in range(NCH)], writes=[psu.t[0]])
                        psm = next_ps()
                        for tt in range(4):
                            em.mm1(psm.ap[:, tt * 128:(tt + 1) * 128], vn.ap[:, tt, g * 128:(g + 1) * 128], wsT[e_].ap[:, g, :], True, True,
                                   reads=[vn.t[tt], wsT[e_].t[0]], writes=[psm.t[0]])
                        gu_ = gu[k_ % 2]
                        tm_ = tm[k_ % 2]
                        k_ += 1
                        em.op("act", lambda e, psu=psu, gu_=gu_: e.activation(out=gu_.ap[:, :], in_=psu.ap[:, :], func=AF.Gelu_apprx_tanh), reads=[psu.t[0]], writes=[gu_.t[0]])
                        em.op("dve", lambda e, psm=psm, tm_=tm_, g=g: e.tensor_tensor(
                            out=tm_.ap[:, :].rearrange("p (t i) -> p t i", i=128), in0=psm.ap[:, :].rearrange("p (t i) -> p t i", i=128),
                            in1=bsb.ap[:, g * 128:(g + 1) * 128].unsqueeze(1).to_broadcast([128, 4, 128]), op=ALU.add),
                            reads=[psm.t[0], bsb.t[0]], writes=[tm_.t[0]])
                        em.op("dve", lambda e, tm_=tm_, gu_=gu_, g=g: e.tensor_tensor(out=y.ap[:, g, :], in0=tm_.ap[:, :], in1=gu_.ap[:, :], op=ALU.mult),
                              reads=[tm_.t[0], gu_.t[0]], writes=[y.t[g]])
                for cb in range(4):
                    wv, wt = wblock_k2048(W, 0, 3072 + cb * 256)
                    for j in range(2):
                        c = cb * 2 + j
                        ps = next_ps()
                        em.mm(ps.ap[:, :], [(wv[:, kc, j * 128:(j + 1) * 128], xn.ap[:, kc, :]) for kc in range(NCH)],
                              reads=[wt], pair_reads=[[xn.t[kc]] for kc in range(NCH)], writes=[ps.t[0]])
                        em.op("act", lambda e, ps=ps, c=c: e.copy(out=Bz.ap[:, c, 2:TG + 2], in_=ps.ap[:, :]), reads=[ps.t[0]], writes=[Bz.t[c]])
                for hb in range(4):
                    wv, wt = wblock_k2048(W, 0, 4096 + hb * 256)
                    for j in range(2):
                        c = hb * 2 + j
                        ps = next_ps()
                        em.mm(ps.ap[:, :], [(wv[:, kc, j * 128:(j + 1) * 128], xn.ap[:, kc, :]) for kc in range(NCH)],
                              reads=[wt], pair_reads=[[xn.t[kc]] for kc in range(NCH)], writes=[ps.t[0]])
                        em.op("dve", lambda e, ps=ps, c=c: e.tensor_tensor(out=Bz.ap[:, c, 2:TG + 2], in0=ps.ap[:, :], in1=Bz.ap[:, c, 2:TG + 2], op=ALU.mult),
                              reads=[ps.t[0], Bz.t[c]], writes=[Bz.t[c]])
                em.op("dve", lambda e: e.tensor_copy(out=halo[e_].ap[:, :, :], in_=Bz.ap[:, :, TG:TG + 2]), reads=Bz.t, writes=[halo[e_].t[0]])
                k_ = 0
                for bb in range(4):
                    wv, wt = wblock_k2048(W, 0, 2048 + bb * 256)
                    for j in range(2):
                        c = bb * 2 + j
                        ps = next_ps()
                        em.mm(ps.ap[:, :], [(wv[:, kc, j * 128:(j + 1) * 128], xn.ap[:, kc, :]) for kc in range(NCH)],
                              reads=[wt], pair_reads=[[xn.t[kc]] for kc in range(NCH)], writes=[ps.t[0]])
                        t_ = tm[k_ % 2]
                        k_ += 1
                        cw = lambda k, c=c: gcol(G_CONV + (e_ * 3 + k) * 8, c)
                        em.op("dve", lambda e, t_=t_, c=c, cw=cw: e.tensor_scalar(out=t_.ap[:, :], in0=Bz.ap[:, c, 0:TG], scalar1=cw(0), scalar2=None, op0=ALU.mult),
                              reads=[Bz.t[c], gT.t[0]], writes=[t_.t[0]])
                        em.op("dve", lambda e, t_=t_, c=c, cw=cw: e.scalar_tensor_tensor(out=t_.ap[:, :], in0=Bz.ap[:, c, 1:TG + 1], scalar=cw(1), in1=t_.ap[:, :], op0=ALU.mult, op1=ALU.add),
                              reads=[Bz.t[c], gT.t[0], t_.t[0]], writes=[t_.t[0]])
                        em.op("dve", lambda e, t_=t_, c=c, cw=cw: e.scalar_tensor_tensor(out=t_.ap[:, :], in0=Bz.ap[:, c, 2:TG + 2], scalar=cw(2), in1=t_.ap[:, :], op0=ALU.mult, op1=ALU.add),
                              reads=[Bz.t[c], gT.t[0], t_.t[0]], writes=[t_.t[0]])
                        em.op("dve", lambda e, t_=t_, c=c, ps=ps: e.tensor_tensor(out=y.ap[:, 8 + c, :], in0=ps.ap[:, :], in1=t_.ap[:, :], op=ALU.mult),
                              reads=[ps.t[0], t_.t[0]], writes=[y.t[8 + c]])
                dump("y", lambda c: y.ap[:, c, :], y.t, 16)
                dump("vn", lambda c: vn.ap[:, c // 2, (c % 2) * 512:(c % 2) * 512 + 512], vn.t, 8)
                proj_to_residual(ab_w_out[e_], y)
                em.barrier()

        def mixer_odd(l, tg):
            o_ = l // 2
            W = c_w_in[o_]
            tok0 = tg * TG
            scale = float(192 ** -0.5)
            with ExitStack() as ph:
                cqn = mkbuf(ph, [128, 4, TG], BF16, "cqn", 4)
                oT = mkbuf(ph, [128, NCH, TG], BF16, "oT", NCH)
                with ExitStack() as ph2:
                    cq = mkbuf(ph2, [128, 4, TG], F32, "cq", 4)
                    ckvf = mkbuf(ph2, [128, 2, TG], F32, "ckvf", 2)
                    tk = [mkbuf(ph2, [128, TG], F32, f"tk{i}") for i in range(2)]
                    norm_x(G_MIX + l * 16)
                    for blk in range(3):
                        wv, wt = wblock_k2048(W, 0, blk * 256)
                        for j in range(2):
                            ps = next_ps()
                            em.mm(ps.ap[:, :], [(wv[:, kc, j * 128:(j + 1) * 128], xn.ap[:, kc, :]) for kc in range(NCH)],
                                  reads=[wt], pair_reads=[[xn.t[kc]] for kc in range(NCH)], writes=[ps.t[0]])
                            if blk < 2:
                                c = blk * 2 + j
                                evac_copy(cq.ap[:, c, :], cq.t[c], ps.ap[:, :], ps.t[0])
                            else:
                                evac_copy(ckvf.ap[:, j, :], ckvf.t[j], ps.ap[:, :], ps.t[0])
                    wv, wt = wblock_k2048(W, 0, 768, ncols=64)
                    em.op("act", lambda e, wv=wv: e.copy(out=wv[:, :, 64:128], in_=wv[:, :, 0:64]), reads=[wt], writes=[wt])
                    em.op("dve", lambda e, wv=wv: e.tensor_copy(out=wv[:, :, 128:160], in_=wv[:, :, 32:64]), reads=[wt], writes=[wt])
                    em.op("dve", lambda e, wv=wv: e.tensor_copy(out=wv[:, :, 160:192], in_=wv[:, :, 0:32]), reads=[wt], writes=[wt])
                    em.op("act", lambda e, wv=wv: e.copy(out=wv[:, :, 192:256], in_=wv[:, :, 128:192]), reads=[wt], writes=[wt])
                    psk = next_ps()
                    em.mm(psk.ap[:, :], [(wv[:, kc, 0:128], xn.ap[:, kc, :]) for kc in range(NCH)], reads=[wt], pair_reads=[[xn.t[kc]] for kc in range(NCH)], writes=[psk.t[0]])
                    pskp = next_ps()
                    em.mm(pskp.ap[:, :], [(wv[:, kc, 128:256], xn.ap[:, kc, :]) for kc in range(NCH)], reads=[wt], pair_reads=[[xn.t[kc]] for kc in range(NCH)], writes=[pskp.t[0]])
                    em.op("dve", lambda e: e.tensor_tensor(out=tk[0].ap[:, :], in0=pskp.ap[:, :], in1=St.ap[:, :], op=ALU.mult), reads=[pskp.t[0], St.t[0]], writes=[tk[0].t[0]])
                    em.op("dve", lambda e: e.tensor_tensor(out=tk[1].ap[:, :], in0=psk.ap[:, :], in1=Ct.ap[:, :], op=ALU.mult), reads=[psk.t[0], Ct.t[0]], writes=[tk[1].t[0]])
                    em.op("dve", lambda e: e.tensor_tensor(out=krc[o_].ap[:, tok0:tok0 + TG], in0=tk[0].ap[:, :], in1=tk[1].ap[:, :], op=ALU.add),
                          reads=[tk[0].t[0], tk[1].t[0]], writes=[krc[o_].t[tg]])
                    rmsnorm(cq, 4, 512, G_CQ + o_ * 4, cqn, cqn.t, lambda c: cqn.ap[:, c, :])
                    rmsnorm(ckvf, 2, 256, G_CKV + o_ * 2, ckv[o_], [ckv[o_].t[tg]], lambda c: ckv[o_].ap[:, c, tok0:tok0 + TG])
                    em.barrier()
                with ExitStack() as ph2:
                    Kx = [mkbuf(ph2, [128, SEQ], BF16, f"Kx{i}") for i in range(2)]
                    Vx = mkbuf(ph2, [128, 16, 256], BF16, "Vx")
                    qn_db = [[mkbuf(ph2, [128, TG], BF16, f"qn{d}{i}") for i in range(2)] for d in range(2)]
                    qr_db = [mkbuf(ph2, [128, TG], BF16, f"qr{d}") for d in range(2)]
                    wr_db = [mkbuf(ph2, [128, 4, 128], BF16, f"wr{d}") for d in range(2)]
                    wp_db = [mkbuf(ph2, [128, 4, 128], BF16, f"wp{d}") for d in range(2)]
                    pTs = [mkbuf(ph2, [128, TG], BF16, f"pTs{i}") for i in range(4)]
                    rden = mkbuf(ph2, [128, TG], F32, "rden")
                    t2_one = [mkbuf(ph2, [128, TG], F32, f"t2{i}") for i in range(2)]
                    t2_db = [t2_one, t2_one]
                    nkt = 4 * (tg + 1)
                    pi_ = [0]
                    ps_pool[0] = [0, 1, 2, 7]

                    def q_path(hp):
                        d = hp % 2
                        qn, qr, wr, wp, t2 = qn_db[d], qr_db[d], wr_db[d], wp_db[d], t2_db[d]
                        wv, wt = wblock_generic(c_w_uq[o_], 4, hp * 384, 384)
                        wv4 = wv.rearrange("p k (h c) -> p k h c", c=192)
                        em.op("act", lambda e, wv4=wv4, wr=wr: e.copy(out=wr.ap[:, :, :].rearrange("p k (h c) -> p k h c", c=64), in_=wv4[:, :, :, 128:192]),
                              reads=[wt], writes=[wr.t[0]])
                        wp5 = wp.ap[:, :, :].rearrange("p k (h s c) -> p k h s c", s=2, c=32)
                        em.op("dve", lambda e, wv4=wv4, wp5=wp5: e.tensor_copy(out=wp5[:, :, :, 0, :], in_=wv4[:, :, :, 160:192]), reads=[wt], writes=[wp.t[0]])
                        em.op("dve", lambda e, wv4=wv4, wp5=wp5: e.tensor_copy(out=wp5[:, :, :, 1, :], in_=wv4[:, :, :, 128:160]), reads=[wt], writes=[wp.t[0]])
                        for hh in range(2):
                            ps = next_ps()
                            em.mm(ps.ap[:, :], [(wv[:, kc, hh * 192:hh * 192 + 128], cqn.ap[:, kc, :]) for kc in range(4)], reads=[wt], pair_reads=[[cqn.t[kc]] for kc in range(4)], writes=[ps.t[0]])
                            evac_copy(qn[hh].ap[:, :], qn[hh].t[0], ps.ap[:, :], ps.t[0])
                        psr = next_ps()
                        em.mm(psr.ap[:, :], [(wr.ap[:, kc, :], cqn.ap[:, kc, :]) for kc in range(4)], reads=[wr.t[0]], pair_reads=[[cqn.t[kc]] for kc in range(4)], writes=[psr.t[0]])
                        psp = next_ps()
                        em.mm(psp.ap[:, :], [(wp.ap[:, kc, :], cqn.ap[:, kc, :]) for kc in range(4)], reads=[wp.t[0]], pair_reads=[[cqn.t[kc]] for kc in range(4)], writes=[psp.t[0]])
                        em.op("dve", lambda e, psp=psp, t2=t2: e.tensor_tensor(out=t2[0].ap[:, :], in0=psp.ap[:, :], in1=St.ap[:, :], op=ALU.mult), reads=[psp.t[0], St.t[0]], writes=[t2[0].t[0]])
                        em.op("dve", lambda e, psr=psr, t2=t2: e.tensor_tensor(out=t2[1].ap[:, :], in0=psr.ap[:, :], in1=Ct.ap[:, :], op=ALU.mult), reads=[psr.t[0], Ct.t[0]], writes=[t2[1].t[0]])
                        em.op("dve", lambda e, t2=t2, qr=qr: e.tensor_tensor(out=qr.ap[:, :], in0=t2[0].ap[:, :], in1=t2[1].ap[:, :], op=ALU.add), reads=[t2[0].t[0], t2[1].t[0]], writes=[qr.t[0]])

                    q_path(0)
                    for hp in range(8):
                        qn, qr = qn_db[hp % 2], qr_db[hp % 2]
                        if hp + 1 < 8:
                            q_path(hp + 1)
                        wk, wkt = wblock_generic(c_w_ukv[o_], 2, hp * 512, 512)
                        for hh in range(2):
                            for tc in range(tg + 1):
                                ps = next_ps()
                                em.mm(ps.ap[:, :], [(wk[:, kc, hh * 256:hh * 256 + 128], ckv[o_].ap[:, kc, tc * TG:(tc + 1) * TG]) for kc in range(2)],
                                      reads=[wkt, ckv[o_].t[tc]], writes=[ps.t[0]])
                                evac_copy(Kx[hh].ap[:, tc * TG:(tc + 1) * TG], Kx[hh].t[0], ps.ap[:, :], ps.t[0])
                        for t4 in range(nkt // 2):
                            ps = next_ps()
                            for sub in range(2):
                                tkk = t4 * 2 + sub
                                for hh in range(2):
                                    em.mm(ps.ap[:, sub * 256 + hh * 128: sub * 256 + (hh + 1) * 128],
                                          [(ckv[o_].ap[:, kc, tkk * 128:(tkk + 1) * 128], wk[:, kc, hh * 256 + 128:hh * 256 + 256]) for kc in range(2)],
                                          reads=[wkt, ckv[o_].t[tkk // 4]], writes=[ps.t[0]])
                            evac_copy(Vx.ap[:, t4 * 2:t4 * 2 + 2, :], Vx.t[0], ps.ap[:, :].rearrange("p (s c) -> p s c", c=256), ps.t[0])
                        for hh in range(2):
                            h = hp * 2 + hh
                            pso = PS[3 + hh]
                            psd = PS[5 + hh]
                            pb = 64 * hh

                            def S(kt):
                                c0 = max(0, kt - 4 * tg) * 128
                                ps = next_ps()
                                em.mm(ps.ap[:, c0:TG], [(Kx[hh].ap[:, kt * 128:(kt + 1) * 128], qn[hh].ap[:, c0:TG]),
                                                        (krc[o_].ap[pb:pb + 64, kt * 128:(kt + 1) * 128], qr.ap[pb:pb + 64, c0:TG])],
                                      reads=[Kx[hh].t[0], qn[hh].t[0], krc[o_].t[kt // 4], qr.t[0]], writes=[ps.t[0]])
                                p_ = pTs[pi_[0] % 4]
                                pi_[0] += 1
                                em.op("act", lambda e, ps=ps, p_=p_, c0=c0: e.activation(out=p_.ap[:, c0:TG], in_=ps.ap[:, c0:TG], func=AF.Exp, scale=scale),
                                      reads=[ps.t[0]], writes=[p_.t[0]])
                                if kt >= 4 * tg:
                                    em.op("dve", lambda e, p_=p_, c0=c0: e.memset(p_.ap[64:128, c0:c0 + 64], 0.0), writes=[p_.t[0]])
                                return p_, c0

                            LA = 3
                            pend = [S(k2) for k2 in range(min(LA, nkt))]
                            for kt in range(nkt):
                                if kt + LA < nkt:
                                    pend.append(S(kt + LA))
                                p_, c0 = pend.pop(0)
                                em.mm1(pso.ap[:, c0:TG], Vx.ap[:, kt, hh * 128:(hh + 1) * 128], p_.ap[:, c0:TG], kt == 0, kt == nkt - 1,
                                       reads=[Vx.t[0], p_.t[0]], writes=[pso.t[0]])
                                em.mm1(psd.ap[:, c0:TG], ones.ap[:, :], p_.ap[:, c0:TG], kt == 0, kt == nkt - 1,
                                       reads=[ones.t[0], p_.t[0]], writes=[psd.t[0]])
                            em.op("dve", lambda e, psd=psd: e.reciprocal(out=rden.ap[:, :], in_=psd.ap[:, :]), reads=[psd.t[0]], writes=[rden.t[0]])
                            em.op("dve", lambda e, pso=pso, h=h: e.tensor_tensor(out=oT.ap[:, h, :], in0=pso.ap[:, :], in1=rden.ap[:, :], op=ALU.mult),
                                  reads=[pso.t[0], rden.t[0]], writes=[oT.t[h]])
                    em.barrier()
                    ps_pool[0] = list(range(7))
                proj_to_residual(c_w_out[o_], oT)
                em.barrier()

        def rope_tables(ph, s, tg):
            tok0 = tg * TG
            em.dma("pool", posi.ap[:, :], pos_d[s:s + 1, tok0:tok0 + TG].partition_broadcast(128), s_pb[2], writes=[posi.t[0]])
            a = mkbuf(ph, [128, TG], F32, "ra")
            b = mkbuf(ph, [128, TG], F32, "rb")
            ki = mkbuf(ph, [128, TG], I32, "rki")
            dv = lambda fn, reads, writes: em.op("dve", fn, reads=reads, writes=writes)
            dv(lambda e: e.tensor_copy(out=a.ap[:, :], in_=posi.ap[:, :]), [posi.t[0]], [a.t[0]])
            dv(lambda e: e.tensor_scalar(out=a.ap[:, :], in0=a.ap[:, :], scalar1=ropec.ap[:, 0:1], scalar2=None, op0=ALU.mult), [a.t[0], ropec.t[0]], [a.t[0]])
            dv(lambda e: e.tensor_scalar(out=b.ap[:, :], in0=a.ap[:, :], scalar1=float(1.0 / (2 * np.pi)), scalar2=None, op0=ALU.mult), [a.t[0]], [b.t[0]])
            dv(lambda e: e.tensor_copy(out=ki.ap[:, :], in_=b.ap[:, :]), [b.t[0]], [ki.t[0]])
            dv(lambda e: e.tensor_copy(out=b.ap[:, :], in_=ki.ap[:, :]), [ki.t[0]], [b.t[0]])
            dv(lambda e: e.scalar_tensor_tensor(out=a.ap[:, :], in0=b.ap[:, :], scalar=-6.28125, in1=a.ap[:, :], op0=ALU.mult, op1=ALU.add), [a.t[0], b.t[0]], [a.t[0]])
            dv(lambda e: e.scalar_tensor_tensor(out=a.ap[:, :], in0=b.ap[:, :], scalar=-0.0019353071795864769, in1=a.ap[:, :], op0=ALU.mult, op1=ALU.add),
               [a.t[0], b.t[0]], [a.t[0]])

            def wrap(t_):
                dv(lambda e: e.tensor_scalar(out=b.ap[:, :], in0=t_.ap[:, :], scalar1=PI, scalar2=-2 * PI, op0=ALU.is_gt, op1=ALU.mult), [t_.t[0]], [b.t[0]])
                dv(lambda e: e.tensor_tensor(out=t_.ap[:, :], in0=t_.ap[:, :], in1=b.ap[:, :], op=ALU.add), [t_.t[0], b.t[0]], [t_.t[0]])
                dv(lambda e: e.tensor_scalar(out=b.ap[:, :], in0=t_.ap[:, :], scalar1=-PI, scalar2=2 * PI, op0=ALU.is_lt, op1=ALU.mult), [t_.t[0]], [b.t[0]])
                dv(lambda e: e.tensor_tensor(out=t_.ap[:, :], in0=t_.ap[:, :], in1=b.ap[:, :], op=ALU.add), [t_.t[0], b.t[0]], [t_.t[0]])

            wrap(a)
            em.op("act", lambda e: e.activation(out=St.ap[:, :], in_=a.ap[:, :], func=AF.Sin), reads=[a.t[0]], writes=[St.t[0]])
            dv(lambda e: e.tensor_scalar(out=St.ap[:, :], in0=St.ap[:, :], scalar1=ropec.ap[:, 1:2], scalar2=None, op0=ALU.mult), [St.t[0], ropec.t[0]], [St.t[0]])
            dv(lambda e: e.tensor_scalar(out=a.ap[:, :], in0=a.ap[:, :], scalar1=PI / 2, scalar2=None, op0=ALU.add), [a.t[0]], [a.t[0]])
            wrap(a)
            em.op("act", lambda e: e.activation(out=Ct.ap[:, :], in_=a.ap[:, :], func=AF.Sin), reads=[a.t[0]], writes=[Ct.t[0]])

        def seq_setup(s):
            with ExitStack() as ph:
                ms = [mkbuf(ph, [128, D], F32, f"ms{i}") for i in range(2)]
                junk = mkbuf(ph, [128, D], BF16, "junk")
                for mt in range(2):
                    em.dma("sp", ms[mt].ap[:, :], mem_d[s, mt * 128:(mt + 1) * 128, :], s_in[mt], writes=[ms[mt].t[0]])
                    sm = small.ap[:, mt:mt + 1]
                    em.op("dve", lambda e, sm=sm: e.memset(sm, 0.0), writes=[small.t[0]])
                    em.op("act", lambda e, mt=mt, sm=sm: e.activation(out=junk.ap[:, :], in_=ms[mt].ap[:, :], func=AF.Square, accum_out=sm),
                          reads=[ms[mt].t[0]], writes=[junk.t[0], small.t[0]])
                    em.op("dve", lambda e, sm=sm: e.tensor_scalar(out=sm, in0=sm, scalar1=1.0 / D, scalar2=EPS, op0=ALU.mult, op1=ALU.add), reads=[small.t[0]], writes=[small.t[0]])
                    em.op("act", lambda e, sm=sm: e.activation(out=sm, in_=sm, func=AF.Sqrt), reads=[small.t[0]], writes=[small.t[0]])
                    em.op("dve", lambda e, sm=sm: e.reciprocal(out=sm, in_=sm), reads=[small.t[0]], writes=[small.t[0]])
                    em.op("dve", lambda e, mt=mt, sm=sm: e.tensor_scalar(out=ms[mt].ap[:, :], in0=ms[mt].ap[:, :], scalar1=sm, scalar2=None, op0=ALU.mult),
                          reads=[ms[mt].t[0], small.t[0]], writes=[ms[mt].t[0]])
                    for c4 in range(4):
                        ps = next_ps()
                        for ci in range(4):
                            c = c4 * 4 + ci
                            em.tr(ps.ap[:, ci * 128:(ci + 1) * 128], ms[mt].ap[:, c * 128:(c + 1) * 128], ident.ap[:, :], reads=[ms[mt].t[0], ident.t[0]], writes=[ps.t[0]])
                        evac_copy(memhT.ap[:, c4 * 4:c4 * 4 + 4, mt * 128:(mt + 1) * 128], memhT.t[0], ps.ap[:, :].rearrange("p (c m) -> p c m", m=128), ps.t[0])
                em.barrier()

        def boundary(prev, nxt):
            with ExitStack() as ph:
                if nxt is not None:
                    s2, tg2 = nxt
                    xs = [mkbuf(ph, [128, D], F32, f"xs{i}") for i in range(4)]
                    for tt in range(4):
                        em.dma("sp", xs[tt].ap[:, :], x_d[s2, tg2 * TG + tt * 128: tg2 * TG + (tt + 1) * 128, :], s_in[tt], writes=[xs[tt].t[0]])
                if prev is not None:
                    s1, tg1 = prev
                    os_ = [mkbuf(ph, [128, D], F32, f"os{i}") for i in range(2)]
                    if prenormed[0]:
                        prenormed[0] = False
                        norm_finish(X, NCH, D, G_FINAL, X.t, lambda c: X.ap[:, c, :], PSN)
                    else:
                        rmsnorm(X, NCH, D, G_FINAL, X, X.t, lambda c: X.ap[:, c, :])
                    for tt in range(4):
                        o_ = os_[tt % 2]
                        for c4 in range(4):
                            ps = next_ps()
                            for ci in range(4):
                                c = c4 * 4 + ci
                                em.tr(ps.ap[:, ci * 128:(ci + 1) * 128], X.ap[:, c, tt * 128:(tt + 1) * 128], ident.ap[:, :], reads=[X.t[c], ident.t[0]], writes=[ps.t[0]])
                            evac_copy(o_.ap[:, c4 * 512:(c4 + 1) * 512], o_.t[0], ps.ap[:, :], ps.t[0])
                        em.dma("sp", out_d[s1, tg1 * TG + tt * 128: tg1 * TG + (tt + 1) * 128, :], o_.ap[:, :], s_out[tt % 2], reads=[o_.t[0]])
                if nxt is not None:
                    rope_tables(ph, s2, tg2)
                    for c in range(NCH):
                        ps = next_ps()
                        for tt in range(4):
                            em.tr(ps.ap[:, tt * 128:(tt + 1) * 128], xs[tt].ap[:, c * 128:(c + 1) * 128], ident.ap[:, :], reads=[xs[tt].t[0], ident.t[0]], writes=[ps.t[0]])
                        evac_copy(X.ap[:, c, :], X.t[c], ps.ap[:, :], ps.t[0])
                        x_chunk_final(c)
                    x_all_final()
                em.barrier()

        seq_ph = []
        for l in range(n_layers):
            if "m" in phases:
                seq_ph.append(("m", l, G_MIX + l * 16))
            if "c" in phases:
                seq_ph.append(("c", l, G_MEMQ + l * 16))
            if "f" in phases:
                seq_ph.append(("f", l, G_FFN + l * 16))
        prev = None
        for s in range(n_seq):
            seq_setup(s)
            for tg in range(n_tg):
                nxt_norm[0] = seq_ph[0][2] if seq_ph else None
                boundary(prev, (s, tg))
                for i, (kind, l, g) in enumerate(seq_ph):
                    nxt_norm[0] = seq_ph[i + 1][2] if i + 1 < len(seq_ph) else "final"
                    if kind == "m":
                        if l % 2 == 0:
                            mixer_even(l, tg)
                        else:
                            mixer_odd(l, tg)
                    elif kind == "c":
                        cross(l, tg)
                    else:
                        ffn(l)
                prev = (s, tg)
        nxt_norm[0] = None
        boundary(prev, None)
        for k_ in s_out + [s_dbg]:
            em.streams["sp"].append(("wait", k_, em.cnt[k_]))

        with nc.Block() as block:
            em.replay(block)
    return nc


def make_consts():
    ident = np.eye(128, dtype=np.float32)
    inv = (10000.0 ** (-np.arange(0, 64, 2, dtype=np.float32) / np.float32(64))).astype(np.float32)
    ropec = np.zeros((128, 2), np.float32)
    for p in range(128):
        ropec[p, 0] = inv[p % 32]
        ropec[p, 1] = -1.0 if (p % 64) < 32 else 1.0
    return ident, ropec


def make_gvec(norm_mix_g, norm_mem_q_g, norm_mem_kv_g, norm_ffn_g, final_norm_g, c_q_norm_g, c_kv_norm_g, b_conv_w):
    parts = [norm_mix_g.reshape(-1, 128), norm_mem_q_g.reshape(-1, 128), norm_mem_kv_g.reshape(-1, 128), norm_ffn_g.reshape(-1, 128),
             final_norm_g.reshape(-1, 128), c_q_norm_g.reshape(-1, 128), c_kv_norm_g.reshape(-1, 128), b_conv_w.reshape(-1, 128)]
    g = np.ascontiguousarray(np.concatenate(parts, axis=0).astype(np.float32))
    assert g.shape == (G_ROWS, 128)
    return g


_NC_CACHE = {}


def kernel(x, mem, positions, norm_mix_g, norm_mem_q_g, norm_mem_kv_g, norm_ffn_g, final_norm_g, ab_w_in, a_v_norm_g, a_w_s, a_b_s,
           b_conv_w, ab_w_out, c_w_in, c_q_norm_g, c_kv_norm_g, c_w_uq, c_w_ukv, c_w_out, m_wq, m_wk, m_wv, m_wo, f_w1, f_w2):
    n_cores = 8
    f = lambda a: np.ascontiguousarray(np.asarray(a, dtype=np.float32))
    ident, ropec = make_consts()
    gvec = make_gvec(f(norm_mix_g), f(norm_mem_q_g), f(norm_mem_kv_g), f(norm_ffn_g), f(final_norm_g), f(c_q_norm_g), f(c_kv_norm_g), f(b_conv_w))
    shared = {
        "gvec": gvec, "ident": ident, "ropec": ropec,
        "ab_w_in": f(ab_w_in), "a_v_norm_g": f(a_v_norm_g), "a_w_s": f(a_w_s), "a_b_s": f(a_b_s).reshape(2, 1024),
        "ab_w_out": f(ab_w_out), "c_w_in": f(c_w_in), "c_w_uq": f(c_w_uq), "c_w_ukv": f(c_w_ukv), "c_w_out": f(c_w_out),
        "m_wq": f(m_wq), "m_wk": f(m_wk), "m_wv": f(m_wv), "m_wo": f(m_wo), "f_w1": f(f_w1), "f_w2": f(f_w2),
    }
    x = f(x)
    mem = f(mem)
    positions = np.ascontiguousarray(np.asarray(positions, dtype=np.int32))
    if "nc" not in _NC_CACHE:
        _NC_CACHE["nc"] = build_program()
    nc = _NC_CACHE["nc"]
    in_maps = []
    for c in range(n_cores):
        m = dict(shared)
        m["x"] = np.ascontiguousarray(x[2 * c:2 * c + 2])
        m["mem"] = np.ascontiguousarray(mem[2 * c:2 * c + 2])
        m["positions"] = np.ascontiguousarray(positions[2 * c:2 * c + 2])
        in_maps.append(m)
    res = run_bass_kernel_spmd(nc, in_maps, core_ids=list(range(n_cores)))
    return np.concatenate([r["out"] for r in res.results], axis=0).astype(np.float32)
```
